# Optimizing a Trainium2 kernel written in Bass

```python
import jax, jax.numpy as jnp
from jax import lax
import numpy as np

D_MODEL = 2048
BATCH = 4
SEQ = 2048
DEPTH = 4
DEC_BATCH = 128
DEC_SEQ = 8
PAST_LEN = 16384
PAGE_SIZE = 128

N_MIXERS = 3
N_LAYERS_A = len(range(0, DEPTH, N_MIXERS))
N_LAYERS_B = len(range(1, DEPTH, N_MIXERS))
N_LAYERS_C = len(range(2, DEPTH, N_MIXERS))

HG_FORGET_DIM = 128
HG_HEADS = D_MODEL // HG_FORGET_DIM
HG_HEAD_V = D_MODEL // HG_HEADS

GLA_HEADS = 4
GLA_DK = D_MODEL // 2
GLA_DV = D_MODEL
GLA_HEAD_K = GLA_DK // GLA_HEADS
GLA_HEAD_V = GLA_DV // GLA_HEADS
GLA_GATE_RANK = 16
GLA_GATE_NORM = 16.0

POOL_WINDOWS = (2, 4, 8, 16)
POOL_GROUPS = len(POOL_WINDOWS)
POOL_GROUP_DIM = D_MODEL // POOL_GROUPS
POOL_BUF = max(POOL_WINDOWS) - 1

D_FF = 5632
CHUNK = 32
EPS = 1e-6

kernel_name = "hgrn2_gla_pool_macaron_decoder_step"


def rmsnorm(x, g):
    xf = x.astype(jnp.float32)
    y = xf * lax.rsqrt(jnp.mean(xf * xf, axis=-1, keepdims=True) + EPS)
    return (y * g.astype(jnp.float32)).astype(x.dtype)


def swiglu(x, w_gate, w_up, w_down):
    return (jax.nn.silu(x @ w_gate) * (x @ w_up)) @ w_down


def gla_chunked(q, k, v, log_f, s0):
    B, T, H, K = q.shape
    C = min(CHUNK, T)
    n = -(-T // C)
    pad = n * C - T

    def chunks(a):
        a = jnp.pad(a.astype(jnp.float32), ((0, 0), (0, pad), (0, 0), (0, 0)))
        return jnp.moveaxis(a.reshape(B, n, C, H, a.shape[-1]), 1, 0)

    causal = jnp.tril(jnp.ones((C, C), dtype=bool))[None, :, :, None, None]

    def step(S, blk):
        qc, kc, vc, gc = blk
        b = jnp.cumsum(gc, axis=1)
        b_last = b[:, -1]
        o_inter = jnp.einsum('bthk,bhkv->bthv', qc * jnp.exp(b), S)
        decay = jnp.exp(jnp.where(causal, b[:, :, None] - b[:, None, :], -jnp.inf))
        att = jnp.einsum('bthk,bshk,btshk->bhts', qc, kc, decay)
        o = o_inter + jnp.einsum('bhts,bshv->bthv', att, vc)
        S = jnp.exp(b_last)[..., None] * S + jnp.einsum(
            'bshk,bshv->bhkv', kc * jnp.exp(b_last[:, None] - b), vc)
        return S, o

    s_fin, o = lax.scan(step, s0.astype(jnp.float32),
                        (chunks(q), chunks(k), chunks(v), chunks(log_f)))
    o = jnp.moveaxis(o, 0, 1).reshape(B, n * C, H, -1)[:, :T]
    return o, s_fin


def hgrn_lower_bounds(logits):
    p = jax.nn.softmax(logits.astype(jnp.float32), axis=0)
    gamma = jnp.cumsum(p, axis=0)
    return gamma - gamma[0]


def hgrn2_mixer(xn, s0, lb, w_in, o_gain, w_out):
    B, T, _ = xn.shape
    q, f, i, g = jnp.split(xn @ w_in, 4, axis=-1)
    ff = f.astype(jnp.float32)
    log_f = jnp.logaddexp(jnp.log(lb), jnp.log1p(-lb) + jax.nn.log_sigmoid(ff))
    k = (1.0 - lb) * jax.nn.sigmoid(-ff)
    heads = lambda a: a.reshape(B, T, HG_HEADS, -1)
    qh = heads(jax.nn.silu(q).astype(jnp.float32)) * (HG_FORGET_DIM ** -0.5)
    o, s = gla_chunked(qh, heads(k), heads(i), heads(log_f), s0)
    o = rmsnorm(o.reshape(B, T, D_MODEL).astype(xn.dtype), o_gain) * jax.nn.silu(g)
    return o @ w_out, s.astype(s0.dtype)


def gla_mixer(xn, s0, w_in, w_gate_up, b_gate, o_gain, w_out):
    B, T, _ = xn.shape
    q, k, v, r, glow = jnp.split(
        xn @ w_in, [GLA_DK, 2 * GLA_DK, 2 * GLA_DK + GLA_DV, 2 * GLA_DK + 2 * GLA_DV], axis=-1)
    log_f = jax.nn.log_sigmoid((glow @ w_gate_up + b_gate).astype(jnp.float32)) / GLA_GATE_NORM
    heads = lambda a: a.reshape(B, T, GLA_HEADS, -1)
    qh = heads(q.astype(jnp.float32)) * (GLA_HEAD_K ** -0.5)
    o, s = gla_chunked(qh, heads(k), heads(v), heads(log_f), s0)
    o = rmsnorm(o.astype(xn.dtype), o_gain.reshape(GLA_HEADS, GLA_HEAD_V))
    o = o.reshape(B, T, GLA_DV) * jax.nn.silu(r)
    return o @ w_out, s.astype(s0.dtype)


def pool_mixer(xn, buf, n_prev, w_group, scale):
    B, T, D = xn.shape
    xe = jnp.concatenate([buf.astype(xn.dtype), xn], axis=1)
    cs = jnp.pad(jnp.cumsum(xe.astype(jnp.float32), axis=1), ((0, 0), (1, 0), (0, 0)))
    t = jnp.arange(T)
    outs = []
    for gi, w in enumerate(POOL_WINDOWS):
        sl = slice(gi * POOL_GROUP_DIM, (gi + 1) * POOL_GROUP_DIM)
        win = cs[:, POOL_BUF + 1:POOL_BUF + 1 + T, sl] - cs[:, POOL_BUF + 1 - w:POOL_BUF + 1 - w + T, sl]
        cnt = jnp.minimum(w, t + 1 + n_prev).astype(jnp.float32)
        outs.append(win / cnt[None, :, None] - xn[:, :, sl].astype(jnp.float32))
    y = jnp.stack(outs, axis=2).astype(xn.dtype)
    y = jnp.einsum('btgc,gcd->btgd', y, w_group).reshape(B, T, D) * scale
    return y, xe[:, -POOL_BUF:]


def trunk(x, st_a, st_b, st_c, n_prev, lb, p):
    new_a, new_b, new_c = [], [], []
    for li in range(DEPTH):
        j = li // N_MIXERS
        x = x + 0.5 * swiglu(rmsnorm(x, p['norm_ffn1'][li]), p['ffn1_w_gate'][li],
                             p['ffn1_w_up'][li], p['ffn1_w_down'][li])
        xn = rmsnorm(x, p['norm_mix'][li])
        kind = li % N_MIXERS
        if kind == 0:
            h, s = hgrn2_mixer(xn, st_a[j], lb[li], p['hgrn_w_in'][j], p['hgrn_o_norm'][j], p['hgrn_w_out'][j])
            new_a.append(s)
        elif kind == 1:
            h, s = gla_mixer(xn, st_b[j], p['gla_w_in'][j], p['gla_w_gate_up'][j], p['gla_b_gate'][j],
                             p['gla_o_norm'][j], p['gla_w_out'][j])
            new_b.append(s)
        else:
            h, s = pool_mixer(xn, st_c[j], n_prev, p['pool_w_group'][j], p['pool_scale'][j])
            new_c.append(s)
        x = x + h
        x = x + 0.5 * swiglu(rmsnorm(x, p['norm_ffn2'][li]), p['ffn2_w_gate'][li],
                             p['ffn2_w_up'][li], p['ffn2_w_down'][li])
    return rmsnorm(x, p['final_norm']), jnp.stack(new_a), jnp.stack(new_b), jnp.stack(new_c)


def setup_inputs(seed: int = 0) -> dict:
    key = jax.random.key(seed)
    keys = iter(jax.random.split(key, 64))

    def nrm(shape, scale):
        return scale * jax.random.normal(next(keys), shape, jnp.float32)

    def gain(shape):
        return 1.0 + nrm(shape, 0.02)

    D = D_MODEL
    return {
        "x_prompt": nrm((BATCH, SEQ, D), 1.0),
        "x_sample": nrm((DEC_BATCH, DEC_SEQ, D), 1.0),
        "state_hgrn": nrm((N_LAYERS_A, DEC_BATCH, HG_HEADS, HG_FORGET_DIM, HG_HEAD_V), 0.5),
        "state_gla": nrm((N_LAYERS_B, DEC_BATCH, GLA_HEADS, GLA_HEAD_K, GLA_HEAD_V), 1.0),
        "state_pool": nrm((N_LAYERS_C, DEC_BATCH, POOL_BUF, D), 1.0),
        "norm_ffn1": gain((DEPTH, D)),
        "ffn1_w_gate": nrm((DEPTH, D, D_FF), D ** -0.5),
        "ffn1_w_up": nrm((DEPTH, D, D_FF), D ** -0.5),
        "ffn1_w_down": nrm((DEPTH, D_FF, D), D_FF ** -0.5),
        "norm_mix": gain((DEPTH, D)),
        "norm_ffn2": gain((DEPTH, D)),
        "ffn2_w_gate": nrm((DEPTH, D, D_FF), D ** -0.5),
        "ffn2_w_up": nrm((DEPTH, D, D_FF), D ** -0.5),
        "ffn2_w_down": nrm((DEPTH, D_FF, D), D_FF ** -0.5),
        "hgrn_lb_logits": nrm((DEPTH, HG_HEADS * HG_FORGET_DIM), 0.1),
        "hgrn_w_in": nrm((N_LAYERS_A, D, 4 * D), D ** -0.5),
        "hgrn_o_norm": gain((N_LAYERS_A, D)),
        "hgrn_w_out": nrm((N_LAYERS_A, D, D), D ** -0.5),
        "gla_w_in": nrm((N_LAYERS_B, D, 2 * GLA_DK + 2 * GLA_DV + GLA_GATE_RANK), D ** -0.5),
        "gla_w_gate_up": nrm((N_LAYERS_B, GLA_GATE_RANK, GLA_DK), GLA_GATE_RANK ** -0.5),
        "gla_b_gate": nrm((N_LAYERS_B, GLA_DK), 0.1),
        "gla_o_norm": gain((N_LAYERS_B, GLA_DV)),
        "gla_w_out": nrm((N_LAYERS_B, GLA_DV, D), GLA_DV ** -0.5),
        "pool_w_group": nrm((N_LAYERS_C, POOL_GROUPS, POOL_GROUP_DIM, POOL_GROUP_DIM), POOL_GROUP_DIM ** -0.5),
        "pool_scale": gain((N_LAYERS_C, D)),
        "final_norm": gain((D,)),
    }


def reference(x_prompt, x_sample, state_hgrn, state_gla, state_pool,
              norm_ffn1, ffn1_w_gate, ffn1_w_up, ffn1_w_down, norm_mix,
              norm_ffn2, ffn2_w_gate, ffn2_w_up, ffn2_w_down,
              hgrn_lb_logits, hgrn_w_in, hgrn_o_norm, hgrn_w_out,
              gla_w_in, gla_w_gate_up, gla_b_gate, gla_o_norm, gla_w_out,
              pool_w_group, pool_scale, final_norm):
    p = {
        'norm_ffn1': norm_ffn1, 'ffn1_w_gate': ffn1_w_gate, 'ffn1_w_up': ffn1_w_up, 'ffn1_w_down': ffn1_w_down,
        'norm_mix': norm_mix,
        'norm_ffn2': norm_ffn2, 'ffn2_w_gate': ffn2_w_gate, 'ffn2_w_up': ffn2_w_up, 'ffn2_w_down': ffn2_w_down,
        'hgrn_w_in': hgrn_w_in, 'hgrn_o_norm': hgrn_o_norm, 'hgrn_w_out': hgrn_w_out,
        'gla_w_in': gla_w_in, 'gla_w_gate_up': gla_w_gate_up, 'gla_b_gate': gla_b_gate,
        'gla_o_norm': gla_o_norm, 'gla_w_out': gla_w_out,
        'pool_w_group': pool_w_group, 'pool_scale': pool_scale, 'final_norm': final_norm,
    }
    lb = hgrn_lower_bounds(hgrn_lb_logits)
    dt = x_prompt.dtype
    z_a = jnp.zeros((N_LAYERS_A, BATCH, HG_HEADS, HG_FORGET_DIM, HG_HEAD_V), dt)
    z_b = jnp.zeros((N_LAYERS_B, BATCH, GLA_HEADS, GLA_HEAD_K, GLA_HEAD_V), dt)
    z_c = jnp.zeros((N_LAYERS_C, BATCH, POOL_BUF, D_MODEL), dt)
    y_prompt, hgrn_p, gla_p, pool_p = trunk(x_prompt, z_a, z_b, z_c, 0, lb, p)
    y_sample, hgrn_s, gla_s, pool_s = trunk(x_sample, state_hgrn, state_gla, state_pool,
                                            min(POOL_BUF, PAST_LEN), lb, p)
    return (y_prompt, y_sample, hgrn_p, gla_p, pool_p, hgrn_s, gla_s, pool_s)
```

```python
import numpy as np
from contextlib import ExitStack
import concourse.bass as bass
import concourse.mybir as mybir
from concourse.bass_utils import run_bass_kernel_spmd

F32 = mybir.dt.float32
BF16 = mybir.dt.bfloat16
AF = mybir.ActivationFunctionType
ALU = mybir.AluOpType

D = 2048
KC = 16
NT = 1152
DFF = 5632
BLKS = [(0, 512), (512, 512), (1024, 128)]
NPART = 11
EPS = 1e-6
NCORES = 8
RG = [[0, 1], [2, 3], [4, 5], [6, 7]]
DBG_HEADS = None

C_ID = 0
C_CM32 = 128
C_CM8 = 256
C_CM128 = 384
C_SM32 = 512
C_SM8 = 516
C_SM128 = 532
C_SEG32 = 533
C_SEG8 = 1045
C_SEG128 = 1173
C_ONES = 1685
C_FLAG = 1813
C_EPS = 1814
C_INVC = 1815
C_INVW = 1879
C_END = 1883

V_NF1, V_NMIX, V_NF2, V_FIN, V_HON, V_GON, V_GB, V_PSC, V_LBL = 0, 4, 8, 12, 13, 15, 16, 17, 18
NV = 22


class Bld:
    def __init__(self, nc):
        self.nc = nc
        self.names = ['pe', 'act', 'dve', 'pool', 'sp']
        self.q = {n: [] for n in self.names}
        self.cnt = {n: 0 for n in self.names}
        self.sem = {n: nc.alloc_semaphore(name="s_" + n) for n in self.names}
        self.waited = {n: {} for n in self.names}
        self.lw = {}
        self.rd = {}
        self.nd = 24
        self.dsem = [nc.alloc_semaphore(name="d%d" % i) for i in range(self.nd)]
        self.dcnt = [0] * self.nd
        self.dnext = {'sp': 0, 'pool': 0}
        self.dbase = {'sp': 0, 'pool': 12}

    def _wait(self, eng, tok):
        sid, sem, val, src = tok
        if eng == 'pe' and src == 'pe':
            return
        if self.waited[eng].get(sid, 0) < val:
            self.waited[eng][sid] = val
            self.q[eng].append(('w', sem, val))

    def _deps(self, eng, r, w):
        for k in r:
            for t in self.lw.get(k, ()):
                self._wait(eng, t)
        for k in w:
            for t in self.lw.get(k, ()):
                self._wait(eng, t)
            for t in self.rd.get(k, {}).values():
                self._wait(eng, t)

    def _commit(self, tok, r, w):
        for k in w:
            self.lw[k] = [tok]
            self.rd[k] = {}
        for k in r:
            if k in w:
                continue
            d = self.rd.setdefault(k, {})
            o = d.get(tok[0])
            if o is None or o[2] < tok[2]:
                d[tok[0]] = tok

    def op(self, eng, fn, r=(), w=(), inc=True):
        self._deps(eng, r, w)
        val = self.cnt[eng] + 1
        tok = (eng, self.sem[eng], val, eng)
        if inc:
            self.cnt[eng] = val
            self.q[eng].append(('o', fn, self.sem[eng], 1))
        else:
            self.q[eng].append(('o', fn, None, 0))
        self._commit(tok, r, w)

    def dma(self, qn, out, in_, r=(), w=()):
        i = self.dbase[qn] + self.dnext[qn]
        self.dnext[qn] = (self.dnext[qn] + 1) % 12
        self._deps(qn, r, w)
        sid = 'd%d' % i
        if self.dcnt[i] > 0:
            self._wait(qn, (sid, self.dsem[i], self.dcnt[i] * 16, 'dma'))
        self.dcnt[i] += 1
        tok = (sid, self.dsem[i], self.dcnt[i] * 16, 'dma')
        self.q[qn].append(('o', (lambda e, o=out, s=in_: e.dma_start(out=o, in_=s)), self.dsem[i], 16))
        self._commit(tok, r, w)

    def fence(self, src, dst):
        toks = []
        for k in src:
            toks += list(self.lw.get(k, ()))
            toks += list(self.rd.get(k, {}).values())
        for k in dst:
            self.lw[k] = list(self.lw.get(k, ())) + toks

    def finish(self):
        for n in self.names:
            if self.cnt[n] > 0:
                self._wait('sp', (n, self.sem[n], self.cnt[n], n))
        for i in range(self.nd):
            if self.dcnt[i] > 0:
                self._wait('sp', ('d%d' % i, self.dsem[i], self.dcnt[i] * 16, 'dma'))

    def replay(self, eng_name, e):
        for it in self.q[eng_name]:
            if it[0] == 'w':
                e.wait_ge(it[1], it[2])
            else:
                ins = it[1](e)
                if it[2] is not None:
                    ins.then_inc(it[2], it[3])


def build_program(stages=None, nlay=4, na=2):
    nc = bass.Bass("TRN2", target_bir_lowering=False)
    es = ExitStack()

    def din(name, shape):
        return nc.dram_tensor(name, list(shape), F32, kind="ExternalInput").ap()

    def dout(name, shape):
        return nc.dram_tensor(name, list(shape), F32, kind="ExternalOutput").ap()

    x_in = din("x_in", [NT, D])
    st_h = din("st_h", [2, 16, 16, 128, 128])
    st_g = din("st_g", [1, 16, 4, 256, 512])
    st_p = din("st_p", [1, 16, 15, D])
    cst_d = din("cst", [128, C_END])
    vecs_d = din("vecs", [128, NV * 16])
    WSH = {"ffn1_w_gate": [nlay, D, DFF], "ffn1_w_up": [nlay, D, DFF], "ffn1_w_down": [nlay, DFF, D],
           "ffn2_w_gate": [nlay, D, DFF], "ffn2_w_up": [nlay, D, DFF], "ffn2_w_down": [nlay, DFF, D],
           "hgrn_w_in": [na, D, 4 * D], "hgrn_w_out": [na, D, D],
           "gla_w_in": [1, D, 6160], "gla_w_gate_up": [1, 16, 1024], "gla_w_out": [1, D, D],
           "pool_w_group": [1, 4, 512, 512]}

    class LazyW(dict):
        def __missing__(self, k):
            v = din(k, WSH[k])
            self[k] = v
            return v
    W = LazyW()

    y_out = dout("y", [NT, D])
    o_hp = dout("o_hp", [2, 16, 128, 128])
    o_gp = dout("o_gp", [1, 4, 256, 512])
    o_pp = dout("o_pp", [1, 15, D])
    o_hs = dout("o_hs", [2, 16, 16, 128, 128])
    o_gs = dout("o_gs", [1, 16, 4, 256, 512])
    o_ps = dout("o_ps", [1, 16, 15, D])

    og_d = nc.dram_tensor("og_d", [128, 16, NT], BF16).ap()
    cc_in = [nc.dram_tensor("cc_in%d" % i, [2048, 128], F32) for i in range(2)]
    cc_out = [nc.dram_tensor("cc_out%d" % i, [4096, 128], F32) for i in range(2)]
    ccg_in = nc.dram_tensor("ccg_in", [1024, 512], F32)
    ccg_out = nc.dram_tensor("ccg_out", [2048, 512], F32)
    ccp_in = nc.dram_tensor("ccp_in", [128, 16 * 15], F32)
    ccp_out = nc.dram_tensor("ccp_out", [256, 16 * 15], F32)

    def sb(name, shape, dt):
        return es.enter_context(nc.sbuf_tensor(name, list(shape), dt))

    xs = sb("xs", [128, KC, NT], F32)
    xb = sb("xb", [128, KC, NT], BF16)
    BIGN = 33792 - 2048
    big = sb("big", [128, BIGN], BF16)
    fa = sb("fa", [128, 2048], F32)
    osq = sb("osq", [128, NT], F32)
    rsa = sb("rsa", [128, NT], F32)
    Sf = [sb("Sf%d" % i, [128, 2, 512], F32) for i in range(2)]
    cst = sb("cst_sb", [128, C_END], F32)
    vecs = sb("vecs_sb", [128, NV, 16], F32)
    lbt = sb("lbt", [128, 4, 16], F32)
    ebl = sb("ebl", [128, 2, 16], F32)
    tsum = sb("tsum", [128, 2, 2], F32)
    SQB = sb("sqb", [128, 512], BF16)[:, :]
    ONB = sb("onb", [128, 128], BF16)[:, :]
    ps = [es.enter_context(nc.psum_tensor("ps%d" % i, [128, 512], F32)) for i in range(8)]

    b = Bld(nc)
    T = [fa[:, i * 512:(i + 1) * 512] for i in range(4)]
    TK = ['T0', 'T1', 'T2', 'T3']

    def bslice(off, n):
        return big[:, off:off + n]
    HB = [bslice(0, 4608), bslice(4608, 4608)]
    WD = [bslice(9216 + i * 2048, 2048) for i in range(7)]
    WGU = [bslice(9216 + 14336 + i * 2048, 2048) for i in range(4)]
    FFN_KEYS = ['hb0', 'hb1'] + ['wd%d' % i for i in range(7)] + ['wgu%d' % i for i in range(4)]
    WR = [bslice(i * 2048, 2048) for i in range(6)]
    mo = 6 * 2048
    QE = bslice(mo, 1024); mo += 1024
    KD = bslice(mo, 1024); mo += 1024
    K2T = bslice(mo, 1024); mo += 1024
    VT = [bslice(mo + i * 512, 512) for i in range(2)]; mo += 1024
    ATT = bslice(mo, 128); mo += 128
    K2M = [bslice(mo + i * 256, 256) for i in range(4)]; mo += 1024
    SB_ = [bslice(mo + i * 1024, 1024) for i in range(2)]; mo += 2048
    GLB = bslice(mo, 512); mo += 512
    OGH = bslice(mo, 4608); mo += 4608
    PB = bslice(mo, 3840); mo += 3840
    WGUB = bslice(mo, 1024); mo += 1024
    YP = OGH
    assert mo <= BIGN, mo
    MIX_KEYS = ['wr%d' % i for i in range(6)] + ['qe', 'kd', 'k2t', 'vt0', 'vt1', 'att', 'k2m0', 'k2m1', 'k2m2', 'k2m3',
                                                'sb0', 'sb1', 'glb', 'ogh', 'wgub', 'yp', 'pb', 'vta'] + ['sb0_%d' % i for i in range(8)]

    cI = cst[:, C_ID:C_ID + 128]
    ones_f = cst[:, C_ONES:C_ONES + 128]
    flag = cst[:, C_FLAG:C_FLAG + 1]
    epsc = cst[:, C_EPS:C_EPS + 1]

    bank = [0]

    held = set()

    def nb(excl=()):
        while True:
            i = bank[0]
            bank[0] = (i + 1) % 8
            if i not in excl and i not in held:
                return i

    def mm(out, lhsT, rhs, start, stop, r, w, inc):
        b.op('pe', lambda e: e.matmul(out, lhsT, rhs, start=start, stop=stop), r=r, w=w, inc=inc)

    def tr(out, in_, r, w):
        b.op('pe', lambda e: e.transpose(out, in_, cI), r=r + ['cst'], w=w)

    def act(out, in_, func, r, w, scale=None, bias=None):
        kw = {}
        if scale is not None:
            kw['scale'] = scale
        if bias is not None:
            kw['bias'] = bias
        b.op('act', lambda e: e.activation(out, in_, func, **kw), r=r, w=w)

    def tt(out, in0, in1, op, r, w):
        b.op('dve', lambda e: e.tensor_tensor(out, in0, in1, op), r=r, w=w)

    def ts(out, in0, s1, s2, op0, op1, r, w):
        if op1 is None:
            b.op('dve', lambda e: e.tensor_scalar(out, in0, s1, None, op0), r=r, w=w)
        else:
            b.op('dve', lambda e: e.tensor_scalar(out, in0, s1, s2, op0, op1), r=r, w=w)

    def stt(out, in0, sc, in1, op0, op1, r, w):
        b.op('dve', lambda e: e.scalar_tensor_tensor(out, in0, sc, in1, op0, op1), r=r, w=w)

    def cp(eng, out, in_, r, w):
        if eng == 'act':
            b.op('act', lambda e: e.activation(out, in_, AF.Copy), r=r, w=w)
        else:
            b.op('dve', lambda e: e.tensor_copy(out, in_), r=r, w=w)

    def wtile(slot):
        return slot.rearrange("p (kc n) -> p kc n", n=128)

    def load_coltile(slot, key, wsrc2d, c0, ncols=128):
        src = wsrc2d.rearrange("(kc p) n -> p kc n", p=128)[:, :, c0:c0 + ncols]
        kcn = wsrc2d.shape[0] // 128
        dst = slot[:, 0:kcn * ncols].rearrange("p (kc n) -> p kc n", n=ncols)
        b.dma('pool', dst, src, r=[], w=[key])
        return dst

    b.dma('sp', cst[:, :], cst_d[:, :], w=['cst'])
    b.dma('sp', vecs[:, :, :], vecs_d.rearrange("p (v k) -> p v k", k=16), w=['vecs'])

    def vec(i):
        return vecs[:, i, :]

    def lower_bounds():
        L = [vec(V_LBL + i) for i in range(4)]
        m = T[0][:, 0:16]; e = [T[1][:, i * 16:(i + 1) * 16] for i in range(4)]; s = T[0][:, 16:32]
        tt(m, L[0], L[1], ALU.max, r=['vecs'], w=['T0'])
        tt(m, m, L[2], ALU.max, r=['vecs', 'T0'], w=['T0'])
        tt(m, m, L[3], ALU.max, r=['vecs', 'T0'], w=['T0'])
        for i in range(4):
            tt(e[i], L[i], m, ALU.subtract, r=['vecs', 'T0'], w=['T1'])
            act(e[i], e[i], AF.Exp, r=['T1'], w=['T1'])
        tt(s, e[0], e[1], ALU.add, r=['T1'], w=['T0'])
        tt(s, s, e[2], ALU.add, r=['T1', 'T0'], w=['T0'])
        tt(s, s, e[3], ALU.add, r=['T1', 'T0'], w=['T0'])
        b.op('dve', lambda en: en.reciprocal(s, s), r=['T0'], w=['T0'])
        tt(lbt[:, 3, :], e[0], s, ALU.mult, r=['T0', 'T1'], w=['lbt'])
        ts(lbt[:, 2, :], lbt[:, 3, :], -1.0, 1.0, ALU.mult, ALU.add, r=['lbt'], w=['lbt'])
        b.op('dve', lambda en: en.memset(lbt[:, 0, :], 0.0), w=['lbt'])
        b.op('dve', lambda en: en.memset(lbt[:, 1, :], 1.0), w=['lbt'])

    def load_x():
        for tt_ in range(9):
            b.dma('sp', fa[:, :], x_in[tt_ * 128:(tt_ + 1) * 128, :], w=TK)
            for g4 in range(4):
                bk = nb()
                for i in range(4):
                    kc = g4 * 4 + i
                    tr(ps[bk][:, i * 128:(i + 1) * 128], fa[:, kc * 128:(kc + 1) * 128], r=TK, w=['ps%d' % bk])
                cp('act' if g4 % 2 else 'dve', xs[:, g4 * 4:(g4 + 1) * 4, tt_ * 128:(tt_ + 1) * 128],
                   ps[bk][:, :].rearrange("p (a n) -> p a n", n=128), r=['ps%d' % bk], w=['xs'])

    def rms_rstd(blk, dst, dkey, src_fn, src_keys, nchunk, scale_n):
        off, L = blk
        bk = nb()
        for kc in range(nchunk):
            act(SQB[:, 0:L], src_fn(kc), AF.Square, r=src_keys, w=['sqb'])
            mm(ps[bk][:, 0:L], ONB, SQB[:, 0:L], kc == 0, kc == nchunk - 1, r=['sqb', 'onb'], w=['ps%d' % bk],
               inc=True)
        act(dst, ps[bk][:, 0:L], AF.Ln, r=['ps%d' % bk, 'cst'], w=[dkey], scale=1.0 / scale_n, bias=epsc)
        act(dst, dst, AF.Exp, r=[dkey], w=[dkey], scale=-0.5)

    def norm_to_xb(gv):
        for blk in BLKS:
            off, L = blk
            rms_rstd(blk, T[0][:, 0:L], 'T0', lambda kc: xs[:, kc, off:off + L], ['xs'], KC, float(D))
            for kc in range(KC):
                stt(xb[:, kc, off:off + L], xs[:, kc, off:off + L], vec(gv)[:, kc:kc + 1], T[0][:, 0:L],
                    ALU.mult, ALU.mult, r=['xs', 'vecs', 'T0'], w=['xb'])

    def ffn(li, which):
        wg = W["ffn%d_w_gate" % which][li]
        wu = W["ffn%d_w_up" % which][li]
        wd = W["ffn%d_w_down" % which][li]
        norm_to_xb((V_NF1 if which == 1 else V_NF2) + li)
        b.fence(MIX_KEYS, FFN_KEYS)
        gi = [0]
        di = [0]
        for part in range(NPART):
            hb = HB[part % 2]
            hk = 'hb%d' % (part % 2)
            hv = hb.rearrange("p (f t) -> p f t", t=NT)
            wds = []
            for fi in range(4):
                f = part * 4 + fi
                tiles = []
                for wsrc in (wg, wu):
                    s = gi[0] % 4
                    gi[0] += 1
                    tiles.append((load_coltile(WGU[s], 'wgu%d' % s, wsrc, f * 128), 'wgu%d' % s))
                s = di[0] % 7
                di[0] += 1
                b.dma('pool', WD[s], wd[f * 128:(f + 1) * 128, :], w=['wd%d' % s])
                wds.append((WD[s], 'wd%d' % s))
                banks = []
                for m in range(2):
                    wt, wk = tiles[m]
                    bks = [nb() for _ in range(3)]
                    banks.append(bks)
                    for kc in range(KC):
                        for bi, (off, L) in enumerate(BLKS):
                            mm(ps[bks[bi]][:, 0:L], wt[:, kc, :], xb[:, kc, off:off + L], kc == 0, kc == KC - 1,
                               r=[wk, 'xb'], w=['ps%d' % bks[bi]], inc=(kc == KC - 1))
                for bi, (off, L) in enumerate(BLKS):
                    tkey = TK[1 + (bi % 2)]
                    tmp = T[1 + (bi % 2)][:, 0:L]
                    act(tmp, ps[banks[0][bi]][:, 0:L], AF.Silu, r=['ps%d' % banks[0][bi]], w=[tkey])
                    tt(hv[:, fi, off:off + L], tmp, ps[banks[1][bi]][:, 0:L], ALU.mult,
                       r=[tkey, 'ps%d' % banks[1][bi]], w=[hk])
            for oc in range(KC):
                for bi, (off, L) in enumerate(BLKS):
                    bk = nb()
                    for fi in range(4):
                        wt, wk = wds[fi]
                        mm(ps[bk][:, 0:L], wt[:, oc * 128:(oc + 1) * 128], hv[:, fi, off:off + L], fi == 0, fi == 3,
                           r=[wk, hk], w=['ps%d' % bk], inc=(fi == 3))
                    stt(xs[:, oc, off:off + L], ps[bk][:, 0:L], 0.5, xs[:, oc, off:off + L], ALU.mult, ALU.add,
                        r=['ps%d' % bk, 'xs'], w=['xs'])
        b.fence(FFN_KEYS, MIX_KEYS)

    def rec_tile(nkc, nvc, tl, tile_cols, vt, vtk, subs, cmask, smask_col0, ebl_col0, state_only,
                 get_state, put_state, o_sink):
        V = nvc * 128
        c0 = tile_cols
        qe3 = QE[:, 0:nkc * 512].rearrange("p (k t) -> p k t", t=512)
        kd3 = KD[:, 0:nkc * 512].rearrange("p (k t) -> p k t", t=512)
        k2t3 = K2T.rearrange("p (a n) -> p a n", n=256)
        nsub, slen = subs
        bo = None
        if not state_only:
            ba = nb()
            for kc in range(nkc):
                mm(ps[ba][:, 0:128], kd3[:, kc, c0:c0 + 128], qe3[:, kc, c0:c0 + 128], kc == 0, kc == nkc - 1,
                   r=['kd', 'qe'], w=['ps%d' % ba], inc=(kc == nkc - 1))
            tt(ATT, ps[ba][:, 0:128], cmask, ALU.mult, r=['ps%d' % ba, 'cst'], w=['att'])
            bo = []
            for _ in range(nvc):
                bo.append(nb(excl=bo))
            for vc in range(nvc):
                mm(ps[bo[vc]][:, 0:128], vt[:, vc * 128:(vc + 1) * 128], ATT, True, False,
                   r=[vtk, 'att'], w=['ps%d' % bo[vc]], inc=True)
        for j in range(nsub):
            S, Sk, Sb, Sbk = get_state(j)
            cj = c0 + j * slen
            if not state_only:
                for vc in range(nvc):
                    for kc in range(nkc):
                        last = (j == nsub - 1) and (kc == nkc - 1)
                        mm(ps[bo[vc]][:, j * slen:(j + 1) * slen],
                           Sb[:, kc * V + vc * 128: kc * V + (vc + 1) * 128],
                           qe3[:, kc, cj:cj + slen], False, last,
                           r=[Sbk, 'qe'], w=['ps%d' % bo[vc]], inc=True)
            for kc in range(nkc):
                m = K2M[(j * nkc + kc) % 4]
                mk = 'k2m%d' % ((j * nkc + kc) % 4)
                ts(m[:, 0:128], k2t3[:, tl, kc * 128:(kc + 1) * 128], cst[:, smask_col0 + j:smask_col0 + j + 1], None,
                   ALU.mult, None, r=['k2t', 'cst'], w=[mk])
                bs = nb(excl=(bo or ()))
                mm(ps[bs][:, 0:V], m[:, 0:128], vt[:, 0:V], True, True, r=[mk, vtk], w=['ps%d' % bs], inc=True)
                S2, S2k = put_state(j)
                stt(S2[:, kc, 0:V], S[:, kc, 0:V], ebl[:, kc, ebl_col0 + j:ebl_col0 + j + 1], ps[bs][:, 0:V],
                    ALU.mult, ALU.add, r=[Sk, 'ebl', 'ps%d' % bs], w=[S2k])
            put_state(j, done=True)
        if not state_only:
            for vc in range(nvc):
                o_sink(vc, ps[bo[vc]][:, 0:128], 'ps%d' % bo[vc])
        return bo or []

    def rec_tile_prompt(nkc, nvc, tl, c0, vt, vtk, nsub, slen, cmask, smask_col0, ebl_col0, S, Sk, sbslot, g0, o_sink):
        V = nvc * 128
        qe3 = QE[:, 0:nkc * 512].rearrange("p (k t) -> p k t", t=512)
        kd3 = KD[:, 0:nkc * 512].rearrange("p (k t) -> p k t", t=512)
        k2t3 = K2T.rearrange("p (a n) -> p a n", n=256)
        cap = 512 // V
        nreg = nsub * nkc
        bd = []
        for _ in range((nreg + cap - 1) // cap):
            bd.append(nb(excl=bd))
        reg = []
        for j in range(nsub):
            for kc in range(nkc):
                idx = j * nkc + kc
                m = K2M[idx % 4]
                mk = 'k2m%d' % (idx % 4)
                if nsub > 1:
                    ts(m[:, 0:128], k2t3[:, tl, kc * 128:(kc + 1) * 128], cst[:, smask_col0 + j:smask_col0 + j + 1],
                       None, ALU.mult, None, r=['k2t', 'cst'], w=[mk])
                    lhs, lk = m[:, 0:128], mk
                else:
                    lhs, lk = k2t3[:, tl, kc * 128:(kc + 1) * 128], 'k2t'
                bk_ = bd[idx // cap]
                ra = ps[bk_][:, (idx % cap) * V:(idx % cap + 1) * V]
                mm(ra, lhs, vt[:, 0:V], True, True, r=[lk, vtk], w=['ps%d' % bk_], inc=True)
                reg.append((ra, 'ps%d' % bk_))
        for j in range(nsub):
            for kc in range(nkc):
                ra, rk = reg[j * nkc + kc]
                stt(S[:, kc, 0:V], S[:, kc, 0:V], ebl[:, kc, ebl_col0 + j:ebl_col0 + j + 1], ra,
                    ALU.mult, ALU.add, r=[Sk, 'ebl', rk], w=[Sk])
            sl, slk = sbslot(g0 + j + 1)
            if nkc == 1:
                cp('act', sl[:, 0:V], S[:, 0, 0:V], r=[Sk], w=[slk])
            else:
                cp('act', sl[:, 0:nkc * V], S[:, :, :].rearrange("p a v -> p (a v)"), r=[Sk], w=[slk])
        ba = nb(excl=bd)
        for kc in range(nkc):
            mm(ps[ba][:, 0:128], kd3[:, kc, c0:c0 + 128], qe3[:, kc, c0:c0 + 128], kc == 0, kc == nkc - 1,
               r=['kd', 'qe'], w=['ps%d' % ba], inc=(kc == nkc - 1))
        tt(ATT, ps[ba][:, 0:128], cmask, ALU.mult, r=['ps%d' % ba, 'cst'], w=['att'])
        bo = []
        for _ in range(nvc):
            bo.append(nb(excl=bo + bd))
        for vc in range(nvc):
            mm(ps[bo[vc]][:, 0:128], vt[:, vc * 128:(vc + 1) * 128], ATT, True, False,
               r=[vtk, 'att'], w=['ps%d' % bo[vc]], inc=True)
        for j in range(nsub):
            sl, slk = sbslot(g0 + j)
            cj = c0 + j * slen
            for vc in range(nvc):
                for kc in range(nkc):
                    last = (j == nsub - 1) and (kc == nkc - 1)
                    mm(ps[bo[vc]][:, j * slen:(j + 1) * slen], sl[:, kc * V + vc * 128: kc * V + (vc + 1) * 128],
                       qe3[:, kc, cj:cj + slen], False, last, r=[slk, 'qe'], w=['ps%d' % bo[vc]], inc=True)
        for vc in range(nvc):
            o_sink(vc, ps[bo[vc]][:, 0:128], 'ps%d' % bo[vc])
        return bo

    def pass1_decay(kc_i, blk_i, L, kk_ap, kk_keys, lf_scale):
        b.op('dve', lambda e: e.tensor_tensor_scan(T[3][:, 0:L], cst[:, C_ONES:C_ONES + 1].broadcast_to([128, L]),
                                                   T[2][:, 0:L], 0.0, ALU.mult, ALU.add), r=['cst', 'T2'], w=['T3'])
        if blk_i == 1:
            cp('dve', tsum[:, kc_i, 0:1], T[3][:, L - 1:L], r=['T3'], w=['tsum'])
            sc_ = tsum[:, kc_i, 0:1]
        else:
            tt(tsum[:, kc_i, 1:2], T[3][:, L - 1:L], tsum[:, kc_i, 0:1], ALU.add, r=['T3', 'tsum'], w=['tsum'])
            sc_ = tsum[:, kc_i, 1:2]
        ts(T[3][:, 0:L], T[3][:, 0:L], sc_, None, ALU.subtract, None, r=['T3', 'tsum'], w=['T3'])
        act(T[3][:, 0:L], T[3][:, 0:L], AF.Exp, r=['T3'], w=['T3'], scale=-lf_scale)
        tt(T[1][:, 0:L], kk_ap, T[3][:, 0:L], ALU.mult, r=kk_keys + ['T3'], w=['T1'])

    def decay_chain(nkc_i, L, seg_col, slen, kk_ap, kk_keys, lf_scale):
        nseg = L // slen
        kd3 = KD[:, 0:2 * 512].rearrange("p (k t) -> p k t", t=512)
        b.op('dve', lambda e: e.tensor_tensor_scan(T[3][:, 0:L], cst[:, seg_col:seg_col + L], T[2][:, 0:L], 0.0,
                                                   ALU.mult, ALU.add), r=['cst', 'T2'], w=['T3'])
        act(T[2][:, 0:L], T[3][:, 0:L], AF.Exp, r=['T3'], w=['T2'], scale=lf_scale)
        act(T[3][:, 0:L], T[3][:, 0:L], AF.Exp, r=['T3'], w=['T3'], scale=-lf_scale)
        tt(T[3][:, 0:L], kk_ap, T[3][:, 0:L], ALU.mult, r=kk_keys + ['T3'], w=['T3'])
        cp('act', kd3[:, nkc_i, 0:L], T[3][:, 0:L], r=['T3'], w=['kd'])
        ebv = T[2][:, 0:L].rearrange("p (s l) -> p s l", l=slen)
        cp('dve', ebl[:, nkc_i, 0:nseg], ebv[:, :, slen - 1], r=['T2'], w=['ebl'])
        tt(T[1][:, 0:L].rearrange("p (s l) -> p s l", l=slen), T[3][:, 0:L].rearrange("p (s l) -> p s l", l=slen),
           ebv[:, :, slen - 1:slen].broadcast_to([128, nseg, slen]), ALU.mult, r=['T3', 'T2'], w=['T1'])

    def k2_transposes(nkc_i, L):
        k2t3 = K2T.rearrange("p (a n) -> p a n", n=256)
        nt_ = L // 128
        bk = nb()
        for t_ in range(nt_):
            tr(ps[bk][:, t_ * 128:(t_ + 1) * 128], T[1][:, t_ * 128:(t_ + 1) * 128], r=['T1'], w=['ps%d' % bk])
        cp('act', k2t3[:, 0:nt_, nkc_i * 128:(nkc_i + 1) * 128],
           ps[bk][:, 0:nt_ * 128].rearrange("p (a n) -> p a n", n=128), r=['ps%d' % bk], w=['k2t'])

    def proj(dst_bank, wt, wk, off, L):
        for kc in range(KC):
            mm(ps[dst_bank][:, 0:L], wt[:, kc, :], xb[:, kc, off:off + L], kc == 0, kc == KC - 1,
               r=[wk, 'xb'], w=['ps%d' % dst_bank], inc=(kc == KC - 1))

    wri = [0]

    def next_w(wsrc2d, c0):
        s = wri[0] % 6
        wri[0] += 1
        return load_coltile(WR[s], 'wr%d' % s, wsrc2d, c0), 'wr%d' % s

    def vtok(wt, wk, off_tok, nvc, vi):
        bk = nb()
        for vc in range(nvc):
            t_, k_ = wt[vc]
            for kc in range(KC):
                mm(ps[bk][:, vc * 128:(vc + 1) * 128], xb[:, kc, off_tok:off_tok + 128], t_[:, kc, :], kc == 0,
                   kc == KC - 1, r=[k_, 'xb'], w=['ps%d' % bk], inc=(kc == KC - 1))
        cp('act', VT[vi][:, 0:nvc * 128], ps[bk][:, 0:nvc * 128], r=['ps%d' % bk], w=['vt%d' % vi])
        return VT[vi], 'vt%d' % vi

    VTALL3 = PB[:, 0:2048].rearrange("p (a v) -> p a v", v=512)

    def vtok_block(wlist, off, L, nvc):
        nt_ = L // 128
        for vc in range(nvc):
            wt, wk = wlist[vc]
            bv = nb()
            proj(bv, wt, wk, off, L)
            ti = 1 if vc % 2 == 0 else 3
            cp('act', T[ti][:, 0:L], ps[bv][:, 0:L], r=['ps%d' % bv], w=[TK[ti]])
            bt = nb()
            for t_ in range(nt_):
                tr(ps[bt][:, t_ * 128:(t_ + 1) * 128], T[ti][:, t_ * 128:(t_ + 1) * 128], r=[TK[ti]], w=['ps%d' % bt])
            cp('dve', VTALL3[:, 0:nt_, vc * 128:(vc + 1) * 128],
               ps[bt][:, 0:nt_ * 128].rearrange("p (a n) -> p a n", n=128), r=['ps%d' % bt], w=['vta'])

        def vt_of(tl):
            return VTALL3[:, tl, 0:nvc * 128], 'vta'
        return vt_of

    def hgrn(li, j):
        win = W["hgrn_w_in"][j]
        wout = W["hgrn_w_out"][j]
        lb_i = 0 if li == 0 else 2
        norm_to_xb(V_NMIX + li)
        b.op('dve', lambda e: e.memset(osq[:, :], 0.0), w=['osq'])
        ogh3 = OGH.rearrange("p (a t) -> p a t", t=NT)
        S0 = Sf[0]
        cci = cc_in[j].ap()
        cco = cc_out[j].ap()

        k2t3_ = K2T.rearrange("p (a n) -> p a n", n=256)

        def sbslot(g):
            return SB_[0][:, (g % 8) * 128:(g % 8) * 128 + 128], 'sb0_%d' % (g % 8)

        def fchain(h, off, L):
            wf, wfk = next_w(win, D + h * 128)
            bf_ = nb()
            proj(bf_, wf, wfk, off, L)
            act(T[0][:, 0:L], ps[bf_][:, 0:L], AF.Sigmoid, r=['ps%d' % bf_], w=['T0'])
            ts(T[0][:, 0:L], T[0][:, 0:L], lbt[:, lb_i + 1, h:h + 1], lbt[:, lb_i, h:h + 1], ALU.mult, ALU.add,
               r=['T0', 'lbt'], w=['T0'])
            act(T[2][:, 0:L], T[0][:, 0:L], AF.Ln, r=['T0'], w=['T2'])
            ts(T[0][:, 0:L], T[0][:, 0:L], -1.0, 1.0, ALU.mult, ALU.add, r=['T0'], w=['T0'])

        def head_p1(h):
            bS = nb()
            held.add(bS)
            first = True
            for bi in (1, 0):
                off, L = BLKS[bi]
                fchain(h, off, L)
                pass1_decay(0, bi, L, T[0][:, 0:L], ['T0'], 1.0)
                k2_transposes(0, L)
                vt_of = vtok_block([next_w(win, 2 * D + h * 128)], off, L, 1)
                for tl in range(4):
                    vt, vtk = vt_of(tl)
                    last = (bi == 0 and tl == 3)
                    mm(ps[bS][:, 0:128], k2t3_[:, tl, 0:128], vt[:, 0:128], first, last, r=['k2t', vtk],
                       w=['ps%d' % bS], inc=True)
                    first = False
            cp('dve', S0[:, 0, 0:128], ps[bS][:, 0:128], r=['ps%d' % bS], w=['Sf0'])
            held.discard(bS)
            for hh in ([h] if DBG_HEADS is None else range(h, 16)):
                b.dma('sp', cci[hh * 128:(hh + 1) * 128, :], S0[:, 0, 0:128], r=['Sf0'], w=['cci'])

        def head_p2(h):
            b.dma('sp', S0[:, 0, 0:128], cco[h * 128:(h + 1) * 128, :], w=['Sf0'], r=['cco'])
            ts(S0[:, 0, 0:128], S0[:, 0, 0:128], flag, None, ALU.mult, None, r=['Sf0', 'cst'], w=['Sf0'])
            sl0, sl0k = sbslot(0)
            cp('act', sl0, S0[:, 0, 0:128], r=['Sf0'], w=[sl0k])
            g = 0
            for bi, (off, L) in enumerate(BLKS):
                samp = (bi == 2)
                slen = 8 if samp else 32
                fchain(h, off, L)
                decay_chain(0, L, C_SEG8 if samp else C_SEG32, slen, T[0][:, 0:L], ['T0'], 1.0)
                k2_transposes(0, L)
                wq, wqk = next_w(win, h * 128)
                bq = nb()
                proj(bq, wq, wqk, off, L)
                act(T[0][:, 0:L], ps[bq][:, 0:L], AF.Silu, r=['ps%d' % bq], w=['T0'])
                stt(QE[:, 0:L], T[0][:, 0:L], float(128 ** -0.5), T[2][:, 0:L], ALU.mult, ALU.mult,
                    r=['T0', 'T2'], w=['qe'])
                wg_, wgk = next_w(win, 3 * D + h * 128)
                bg = nb()
                proj(bg, wg_, wgk, off, L)
                act(ogh3[:, 0, off:off + L], ps[bg][:, 0:L], AF.Silu, r=['ps%d' % bg], w=['ogh'])
                vt_of = vtok_block([next_w(win, 2 * D + h * 128)], off, L, 1)
                for tl in range(L // 128):
                    vt, vtk = vt_of(tl)

                    def o_sink(vc, pso, pk, tl=tl):
                        c_ = off + tl * 128
                        act(T[0][:, 0:128], pso, AF.Square, r=[pk], w=['T0'])
                        tt(osq[:, c_:c_ + 128], osq[:, c_:c_ + 128], T[0][:, 0:128], ALU.add, r=['osq', 'T0'],
                           w=['osq'])
                        stt(ogh3[:, 0, c_:c_ + 128], pso, vec(V_HON + j)[:, h:h + 1], ogh3[:, 0, c_:c_ + 128],
                            ALU.mult, ALU.mult, r=[pk, 'vecs', 'ogh'], w=['ogh'])
                    if not samp:
                        rec_tile_prompt(1, 1, tl, tl * 128, vt, vtk, 4, 32, cst[:, C_CM32:C_CM32 + 128], C_SM32,
                                        tl * 4, S0, 'Sf0', sbslot, g, o_sink)
                        g += 4
                    else:
                        def bufs(jj):
                            if jj // 8 == 0:
                                return Sf[1], 'Sf1', SB_[1], ['sb1']
                            return Sf[0], 'Sf0', SB_[0], ['sb0_%d' % i for i in range(8)]

                        def get_state(jj):
                            S1, S1k, B1, B1k = bufs(jj)
                            s1v = S1[:, :, :].rearrange("p a (s v) -> p (a s) v", v=128)
                            if jj % 8 == 0:
                                g0 = jj
                                src = st_h[j, g0:g0 + 8, h, :, :].rearrange("s k v -> k s v")
                                b.dma('sp', s1v[:, 0:8, :], src, w=[S1k])
                                cp('act', B1[:, 0:1024].rearrange("p (s v) -> p s v", v=128), s1v[:, 0:8, :],
                                   r=[S1k], w=B1k)
                            q_ = jj % 8
                            return (S1[:, q_ // 4:q_ // 4 + 1, (q_ % 4) * 128:(q_ % 4) * 128 + 128], S1k,
                                    B1[:, q_ * 128:(q_ + 1) * 128], B1k[0])

                        def put_state(jj, done=False):
                            S1, S1k, B1, B1k = bufs(jj)
                            s1v = S1[:, :, :].rearrange("p a (s v) -> p (a s) v", v=128)
                            q_ = jj % 8
                            if done:
                                if q_ == 7:
                                    g0 = jj - 7
                                    dst = o_hs[j, g0:g0 + 8, h, :, :].rearrange("s k v -> k s v")
                                    b.dma('sp', dst, s1v[:, 0:8, :], r=[S1k])
                                return None
                            return S1[:, q_ // 4:q_ // 4 + 1, (q_ % 4) * 128:(q_ % 4) * 128 + 128], S1k
                        rec_tile(1, 1, tl, tl * 128, vt, vtk, (16, 8), cst[:, C_CM8:C_CM8 + 128], C_SM8, 0, False,
                                 get_state, put_state, o_sink)
                if bi == 1:
                    b.dma('sp', o_hp[j, h, :, :], S0[:, 0, 0:128], r=['Sf0'])
            for hh in ([h] if DBG_HEADS is None else range(h, 16)):
                b.dma('sp', og_d[:, hh, :], ogh3[:, 0, :], r=['ogh'], w=['og_d'])

        NH = 16 if DBG_HEADS is None else DBG_HEADS
        for h in range(NH):
            head_p1(h)
        b.op('pool', lambda e: e.collective_compute("AllGather", ALU.bypass,
                                                    replica_groups=RG,
                                                    ins=[cci[:, :]], outs=[cco[:, :]]), r=['cci'], w=['cco'])
        for h in range(NH):
            head_p2(h)
        for (off, L) in BLKS:
            bk = nb()
            mm(ps[bk][:, 0:L], ones_f, osq[:, off:off + L], True, True, r=['cst', 'osq'], w=['ps%d' % bk], inc=True)
            act(rsa[:, off:off + L], ps[bk][:, 0:L], AF.Ln, r=['ps%d' % bk, 'cst'], w=['rsa'], scale=1.0 / D,
                bias=epsc)
            act(rsa[:, off:off + L], rsa[:, off:off + L], AF.Exp, r=['rsa'], w=['rsa'], scale=-0.5)
        b.dma('sp', xb[:, :, :], og_d[:, :, :], r=['og_d'], w=['xb'])
        for oc in range(KC):
            wt, wk = next_w(wout, oc * 128)
            for (off, L) in BLKS:
                bk = nb()
                for rt in range(KC):
                    mm(ps[bk][:, 0:L], wt[:, rt, :], xb[:, rt, off:off + L], rt == 0, rt == KC - 1,
                       r=[wk, 'xb'], w=['ps%d' % bk], inc=(rt == KC - 1))
                tt(T[0][:, 0:L], ps[bk][:, 0:L], rsa[:, off:off + L], ALU.mult, r=['ps%d' % bk, 'rsa'], w=['T0'])
                tt(xs[:, oc, off:off + L], xs[:, oc, off:off + L], T[0][:, 0:L], ALU.add, r=['xs', 'T0'], w=['xs'])

    def gla():
        win = W["gla_w_in"][0]
        wout = W["gla_w_out"][0]
        wgu = W["gla_w_gate_up"][0]
        norm_to_xb(V_NMIX + 1)
        b.dma('pool', WGUB[0:16, 0:1024], wgu[:, :], w=['wgub'])
        ogh3 = OGH.rearrange("p (a t) -> p a t", t=NT)
        qe3 = QE[:, 0:1024].rearrange("p (k t) -> p k t", t=512)
        S0 = Sf[0]
        cci = ccg_in.ap()
        cco = ccg_out.ap()

        k2t3_ = K2T.rearrange("p (a n) -> p a n", n=256)

        def sbslot(g):
            return SB_[g % 2][:, 0:1024], 'sb%d' % (g % 2)

        def glow_proj(off, L):
            s_ = wri[0] % 6
            wri[0] += 1
            wl = load_coltile(WR[s_], 'wr%d' % s_, win, 6144, 16)
            bl = nb()
            for kc in range(KC):
                mm(ps[bl][0:16, 0:L], wl[:, kc, :], xb[:, kc, off:off + L], kc == 0, kc == KC - 1,
                   r=['wr%d' % s_, 'xb'], w=['ps%d' % bl], inc=(kc == KC - 1))
            cp('act', GLB[0:16, 0:L], ps[bl][0:16, 0:L], r=['ps%d' % bl], w=['glb'])

        def gk_chain(h, k2, off, L):
            c = h * 256 + k2 * 128
            bgl = nb()
            mm(ps[bgl][:, 0:L], WGUB[0:16, c:c + 128], GLB[0:16, 0:L], True, True, r=['wgub', 'glb'],
               w=['ps%d' % bgl], inc=True)
            act(T[0][:, 0:L], ps[bgl][:, 0:L], AF.Sigmoid, r=['ps%d' % bgl, 'vecs'], w=['T0'],
                bias=vec(V_GB)[:, h * 2 + k2:h * 2 + k2 + 1])
            act(T[2][:, 0:L], T[0][:, 0:L], AF.Ln, r=['T0'], w=['T2'])
            wk_, wkk = next_w(win, 1024 + c)
            bkk = nb()
            proj(bkk, wk_, wkk, off, L)
            cp('dve', T[0][:, 0:L], ps[bkk][:, 0:L], r=['ps%d' % bkk], w=['T0'])

        def head_p1(h):
            bS = [nb()]
            bS.append(nb(excl=bS))
            for x_ in bS:
                held.add(x_)
            first = True
            for bi in (1, 0):
                off, L = BLKS[bi]
                glow_proj(off, L)
                for k2 in range(2):
                    gk_chain(h, k2, off, L)
                    pass1_decay(k2, bi, L, T[0][:, 0:L], ['T0'], 1.0 / 16.0)
                    k2_transposes(k2, L)
                vt_of = vtok_block([next_w(win, 2048 + h * 512 + vc * 128) for vc in range(4)], off, L, 4)
                for tl in range(4):
                    vt, vtk = vt_of(tl)
                    last = (bi == 0 and tl == 3)
                    for k2 in range(2):
                        mm(ps[bS[k2]][:, 0:512], k2t3_[:, tl, k2 * 128:(k2 + 1) * 128], vt[:, 0:512], first, last,
                           r=['k2t', vtk], w=['ps%d' % bS[k2]], inc=True)
                    first = False
            for k2 in range(2):
                cp('dve' if k2 else 'act', S0[:, k2, :], ps[bS[k2]][:, 0:512], r=['ps%d' % bS[k2]], w=['Sf0'])
                held.discard(bS[k2])
            b.dma('sp', cci[h * 256:(h + 1) * 256, :].rearrange("(kc p) v -> p kc v", p=128), S0[:, :, :],
                  r=['Sf0'], w=['cci'])

        def head_p2(h):
            b.dma('sp', S0[:, :, :], cco[h * 256:(h + 1) * 256, :].rearrange("(kc p) v -> p kc v", p=128),
                  w=['Sf0'], r=['cco'])
            ts(S0[:, :, :], S0[:, :, :], flag, None, ALU.mult, None, r=['Sf0', 'cst'], w=['Sf0'])
            sl0, sl0k = sbslot(0)
            cp('act', sl0, S0[:, :, :].rearrange("p a v -> p (a v)"), r=['Sf0'], w=[sl0k])
            g = 0
            for bi, (off, L) in enumerate(BLKS):
                samp = (bi == 2)
                slen = 8 if samp else 128
                glow_proj(off, L)
                for k2 in range(2):
                    c = h * 256 + k2 * 128
                    gk_chain(h, k2, off, L)
                    decay_chain(k2, L, C_SEG8 if samp else C_SEG128, slen, T[0][:, 0:L], ['T0'], 1.0 / 16.0)
                    k2_transposes(k2, L)
                    wq, wqk = next_w(win, c)
                    bq = nb()
                    proj(bq, wq, wqk, off, L)
                    stt(qe3[:, k2, 0:L], ps[bq][:, 0:L], float(256 ** -0.5), T[2][:, 0:L], ALU.mult, ALU.mult,
                        r=['ps%d' % bq, 'T2'], w=['qe'])
                for vc in range(4):
                    wr_, wrk = next_w(win, 4096 + h * 512 + vc * 128)
                    br = nb()
                    proj(br, wr_, wrk, off, L)
                    act(ogh3[:, vc, off:off + L], ps[br][:, 0:L], AF.Silu, r=['ps%d' % br], w=['ogh'])
                vt_of = vtok_block([next_w(win, 2048 + h * 512 + vc * 128) for vc in range(4)], off, L, 4)
                if samp:
                    b.dma('sp', o_gp[0, h, :, :].rearrange("(kc p) v -> p kc v", p=128), S0[:, :, :], r=['Sf0'])
                for tl in range(L // 128):
                    vt, vtk = vt_of(tl)
                    osinks = []

                    def o_sink(vc, pso, pk, tl=tl, osinks=osinks):
                        osinks.append((vc, pso, pk))
                    if not samp:
                        bo_used = rec_tile_prompt(2, 4, tl, tl * 128, vt, vtk, 1, 128, cst[:, C_CM128:C_CM128 + 128],
                                                  C_SM128, tl, S0, 'Sf0', sbslot, g, o_sink)
                        g += 1
                    else:
                        def get_state(jj):
                            q_ = 1 - jj % 2
                            S1 = Sf[q_]
                            src = st_g[0, jj, h, :, :].rearrange("(kc p) v -> p kc v", p=128)
                            b.dma('sp', S1[:, :, :], src, w=['Sf%d' % q_])
                            cp('act', SB_[q_][:, 0:1024], S1[:, :, :].rearrange("p a v -> p (a v)"), r=['Sf%d' % q_],
                               w=['sb%d' % q_])
                            return S1, 'Sf%d' % q_, SB_[q_], 'sb%d' % q_

                        def put_state(jj, done=False):
                            q_ = 1 - jj % 2
                            S1 = Sf[q_]
                            if done:
                                dst = o_gs[0, jj, h, :, :].rearrange("(kc p) v -> p kc v", p=128)
                                b.dma('sp', dst, S1[:, :, :], r=['Sf%d' % q_])
                                return None
                            return S1, 'Sf%d' % q_
                        bo_used = rec_tile(2, 4, tl, tl * 128, vt, vtk, (16, 8), cst[:, C_CM8:C_CM8 + 128], C_SM8, 0,
                                           False, get_state, put_state, o_sink)
                    c_ = off + tl * 128
                    bss = nb(excl=bo_used)
                    for (vc, pso, pk) in osinks:
                        act(SQB[:, vc * 128:(vc + 1) * 128], pso, AF.Square, r=[pk], w=['sqb'])
                    for vc in range(4):
                        mm(ps[bss][:, 0:128], ONB, SQB[:, vc * 128:(vc + 1) * 128], vc == 0, vc == 3,
                           r=['onb', 'sqb'], w=['ps%d' % bss], inc=(vc == 3))
                    act(T[0][:, 0:128], ps[bss][:, 0:128], AF.Ln, r=['ps%d' % bss, 'cst'], w=['T0'],
                        scale=1.0 / 512.0, bias=epsc)
                    act(T[0][:, 0:128], T[0][:, 0:128], AF.Exp, r=['T0'], w=['T0'], scale=-0.5)
                    for (vc, pso, pk) in osinks:
                        stt(T[1][:, 0:128], pso, vec(V_GON)[:, h * 4 + vc:h * 4 + vc + 1], T[0][:, 0:128],
                            ALU.mult, ALU.mult, r=[pk, 'vecs', 'T0'], w=['T1'])
                        tt(ogh3[:, vc, c_:c_ + 128], ogh3[:, vc, c_:c_ + 128], T[1][:, 0:128], ALU.mult,
                           r=['ogh', 'T1'], w=['ogh'])
            for oc in range(KC):
                s_ = wri[0] % 6
                wri[0] += 1
                src = wout[h * 512:(h + 1) * 512, oc * 128:(oc + 1) * 128].rearrange("(r p) n -> p r n", p=128)
                dst = WR[s_][:, 0:512].rearrange("p (r n) -> p r n", n=128)
                b.dma('pool', dst, src, w=['wr%d' % s_])
                for (off, L) in BLKS:
                    bk = nb()
                    for rt in range(4):
                        mm(ps[bk][:, 0:L], dst[:, rt, :], ogh3[:, rt, off:off + L], rt == 0, rt == 3,
                           r=['wr%d' % s_, 'ogh'], w=['ps%d' % bk], inc=(rt == 3))
                    tt(xs[:, oc, off:off + L], xs[:, oc, off:off + L], ps[bk][:, 0:L], ALU.add,
                       r=['xs', 'ps%d' % bk], w=['xs'])

        for h in range(4):
            head_p1(h)
        b.op('pool', lambda e: e.collective_compute("AllGather", ALU.bypass,
                                                    replica_groups=RG,
                                                    ins=[cci[:, :]], outs=[cco[:, :]]), r=['cci'], w=['cco'])
        for h in range(4):
            head_p2(h)

    def pool_mixer():
        wgp = W["pool_w_group"][0]
        cci = ccp_in.ap()
        cco = ccp_out.ap()
        for blk in BLKS:
            off, L = blk
            rms_rstd(blk, rsa[:, off:off + L], 'rsa', lambda kc: xs[:, kc, off:off + L], ['xs'], KC, float(D))
        gm = vec(V_NMIX + 2)
        XE = fa[:, 0:1039]
        XS_ = fa[:, 1040:1040 + 16 * 23]
        xs3 = XS_.rearrange("p (s l) -> p s l", l=23)
        WA = fa[:, 1408:1408 + 320]
        yp3 = YP.rearrange("p (a t) -> p a t", t=NT)
        pb3 = PB.rearrange("p (a t) -> p a t", t=240)

        def tok_major(c0):
            for g4 in range(4):
                bk = nb()
                for i in range(4):
                    kc = g4 * 4 + i
                    stt(osq[:, i * 128:(i + 1) * 128], xs[:, kc, c0:c0 + 128], gm[:, kc:kc + 1], rsa[:, c0:c0 + 128],
                        ALU.mult, ALU.mult, r=['xs', 'vecs', 'rsa'], w=['osq'])
                    tr(ps[bk][:, i * 128:(i + 1) * 128], osq[:, i * 128:(i + 1) * 128], r=['osq'], w=['ps%d' % bk])
                cp('act' if g4 % 2 else 'dve', fa[:, g4 * 512:(g4 + 1) * 512], ps[bk][:, :], r=['ps%d' % bk], w=TK)
        tok_major(1024)
        for s_ in range(16):
            b.dma('sp', o_ps[0, s_, 7:15, :], fa[s_ * 8:(s_ + 1) * 8, :], r=TK)
        b.dma('sp', o_ps[0, :, 0:7, :], st_p[0, :, 8:15, :])
        tok_major(896)
        b.dma('sp', o_pp[0, :, :], fa[113:128, :], r=TK)
        for kc in range(KC):
            stt(WA[:, 0:15], xs[:, kc, 1009:1024], gm[:, kc:kc + 1], rsa[:, 1009:1024], ALU.mult, ALU.mult,
                r=['xs', 'vecs', 'rsa'], w=TK)
            b.dma('sp', cci[:, kc * 15:(kc + 1) * 15], WA[:, 0:15], r=TK, w=['cci'])
        b.op('pool', lambda e: e.collective_compute("AllGather", ALU.bypass,
                                                    replica_groups=RG,
                                                    ins=[cci[:, :]], outs=[cco[:, :]]), r=['cci'], w=['cco'])
        for hh in range(2):
            b.dma('sp', fa[0:120, :], st_p[0, hh * 8:(hh + 1) * 8, :, :].rearrange("s r d -> (s r) d"), w=TK)
            for g4 in range(4):
                bk = nb()
                for i in range(4):
                    kc = g4 * 4 + i
                    b.op('pe', lambda e, o=ps[bk][:, i * 120:(i + 1) * 120], a=fa[0:120, kc * 128:(kc + 1) * 128]:
                         e.transpose(o, a, cst[0:120, C_ID:C_ID + 120]), r=TK + ['cst'], w=['ps%d' % bk])
                cp('act', pb3[:, g4 * 4:(g4 + 1) * 4, hh * 120:(hh + 1) * 120],
                   ps[bk][:, 0:480].rearrange("p (a n) -> p a n", n=120), r=['ps%d' % bk], w=['pb'])
        for g in range(4):
            w_ = (2, 4, 8, 16)[g]
            for ci in range(4):
                kc = g * 4 + ci
                b.dma('sp', XE[:, 0:15], cco[0:128, kc * 15:(kc + 1) * 15], r=['cco'], w=TK)
                ts(XE[:, 0:15], XE[:, 0:15], flag, None, ALU.mult, None, r=TK + ['cst'], w=TK)
                stt(XE[:, 15:1039], xs[:, kc, 0:1024], gm[:, kc:kc + 1], rsa[:, 0:1024], ALU.mult, ALU.mult,
                    r=['xs', 'vecs', 'rsa'], w=TK)
                cp('dve', xs3[:, :, 0:15], pb3[:, kc, :].rearrange("p (s r) -> p s r", r=15), r=['pb'], w=TK)
                stt(xs3[:, :, 15:23], xs[:, kc, 1024:1152].rearrange("p (s l) -> p s l", l=8), gm[:, kc:kc + 1],
                    rsa[:, 1024:1152].rearrange("p (s l) -> p s l", l=8), ALU.mult, ALU.mult,
                    r=['xs', 'vecs', 'rsa'], w=TK)
                acc = osq[:, 0:1024]
                tt(acc, XE[:, 15:1039], XE[:, 14:1038], ALU.add, r=TK, w=['osq'])
                for i in range(2, w_):
                    tt(acc, acc, XE[:, 15 - i:1039 - i], ALU.add, r=TK + ['osq'], w=['osq'])
                stt(yp3[:, ci, 0:1024], acc, 1.0 / w_, XE[:, 15:1039], ALU.mult, ALU.subtract, r=['osq'] + TK,
                    w=['yp'])
                tt(WA[:, 16:32], acc[:, 0:16], cst[:, C_INVC + g * 16:C_INVC + (g + 1) * 16], ALU.mult,
                   r=['osq', 'cst'], w=TK)
                tt(yp3[:, ci, 0:16], WA[:, 16:32], XE[:, 15:31], ALU.subtract, r=TK, w=['yp'])
                accs = osq[:, 1024:1152].rearrange("p (s l) -> p s l", l=8)
                tt(accs, xs3[:, :, 15:23], xs3[:, :, 14:22], ALU.add, r=TK, w=['osq'])
                for i in range(2, w_):
                    tt(accs, accs, xs3[:, :, 15 - i:23 - i], ALU.add, r=TK + ['osq'], w=['osq'])
                stt(yp3[:, ci, 1024:1152].rearrange("p (s l) -> p s l", l=8), accs, 1.0 / w_, xs3[:, :, 15:23],
                    ALU.mult, ALU.subtract, r=['osq'] + TK, w=['yp'])
            for oc4 in range(4):
                oc = g * 4 + oc4
                s = wri[0] % 6
                wri[0] += 1
                src = wgp[g, :, oc4 * 128:(oc4 + 1) * 128].rearrange("(r p) n -> p r n", p=128)
                dst = WR[s][:, 0:512].rearrange("p (r n) -> p r n", n=128)
                b.dma('pool', dst, src, w=['wr%d' % s])
                for (off, L) in BLKS:
                    bk = nb()
                    for rt in range(4):
                        mm(ps[bk][:, 0:L], dst[:, rt, :], yp3[:, rt, off:off + L], rt == 0, rt == 3,
                           r=['wr%d' % s, 'yp'], w=['ps%d' % bk], inc=(rt == 3))
                    stt(xs[:, oc, off:off + L], ps[bk][:, 0:L], vec(V_PSC)[:, oc:oc + 1], xs[:, oc, off:off + L],
                        ALU.mult, ALU.add, r=['ps%d' % bk, 'vecs', 'xs'], w=['xs'])

    def final_store():
        for blk in BLKS:
            off, L = blk
            rms_rstd(blk, rsa[:, off:off + L], 'rsa', lambda kc: xs[:, kc, off:off + L], ['xs'], KC, float(D))
        gf = vec(V_FIN)
        for tt_ in range(9):
            c0 = tt_ * 128
            for g4 in range(4):
                bk = nb()
                for i in range(4):
                    kc = g4 * 4 + i
                    stt(osq[:, i * 128:(i + 1) * 128], xs[:, kc, c0:c0 + 128], gf[:, kc:kc + 1], rsa[:, c0:c0 + 128],
                        ALU.mult, ALU.mult, r=['xs', 'vecs', 'rsa'], w=['osq'])
                    tr(ps[bk][:, i * 128:(i + 1) * 128], osq[:, i * 128:(i + 1) * 128], r=['osq'], w=['ps%d' % bk])
                cp('act' if g4 % 2 else 'dve', fa[:, g4 * 512:(g4 + 1) * 512], ps[bk][:, :], r=['ps%d' % bk],
                   w=[TK[g4]])
            b.dma('sp', y_out[c0:c0 + 128, :], fa[:, :], r=TK)

    cp('dve', ONB, ones_f, r=['cst'], w=['onb'])
    lower_bounds()
    load_x()
    if stages is None:
        stages = []
        for li in range(4):
            stages += ["ffn1:%d" % li, ("hgrn:%d" % li) if li % 3 == 0 else ("gla:%d" % li if li % 3 == 1 else "pool:%d" % li),
                       "ffn2:%d" % li]
    for st in stages:
        kind, li = st.split(":")
        li = int(li)
        if kind == "ffn1":
            ffn(li, 1)
        elif kind == "ffn2":
            ffn(li, 2)
        elif kind == "hgrn":
            hgrn(li, li // 3)
        elif kind == "gla":
            gla()
        elif kind == "pool":
            pool_mixer()
    final_store()
    b.finish()

    with nc.Block() as block:
        @block.tensor
        def _(e):
            b.replay('pe', e)

        @block.scalar
        def _(e):
            b.replay('act', e)

        @block.vector
        def _(e):
            b.replay('dve', e)

        @block.gpsimd
        def _(e):
            b.replay('pool', e)

        @block.sync
        def _(e):
            b.replay('sp', e)
    es.close()
    nc._used_w = list(W.keys())
    return nc


def _consts(core):
    c = np.zeros((128, C_END), np.float32)
    s = np.arange(128)[:, None]
    t = np.arange(128)[None, :]
    c[:, C_ID:C_ID + 128] = np.eye(128)
    c[:, C_CM32:C_CM32 + 128] = ((s // 32 == t // 32) & (s <= t))
    c[:, C_CM8:C_CM8 + 128] = ((s // 8 == t // 8) & (s <= t))
    c[:, C_CM128:C_CM128 + 128] = (s <= t)
    for j in range(4):
        c[:, C_SM32 + j] = (np.arange(128) // 32 == j)
    for j in range(16):
        c[:, C_SM8 + j] = (np.arange(128) // 8 == j)
    c[:, C_SM128] = 1.0
    c[:, C_SEG32:C_SEG32 + 512] = (np.arange(512) % 32 != 0)[None, :]
    c[:, C_SEG8:C_SEG8 + 128] = (np.arange(128) % 8 != 0)[None, :]
    c[:, C_SEG128:C_SEG128 + 512] = (np.arange(512) % 128 != 0)[None, :]
    c[:, C_ONES:C_ONES + 128] = 1.0
    half = core % 2
    c[:, C_FLAG] = float(half)
    c[:, C_EPS] = EPS
    for g, w in enumerate((2, 4, 8, 16)):
        tpos = np.arange(16) + half * 1024
        c[:, C_INVC + g * 16:C_INVC + (g + 1) * 16] = (1.0 / np.minimum(w, tpos + 1))[None, :]
        c[:, C_INVW + g] = 1.0 / w
    return c


def _fm(v):
    return np.ascontiguousarray(np.asarray(v, np.float32).reshape(16, 128).T)


_NC_CACHE = {}


def kernel(**inp):
    f32 = lambda a: np.ascontiguousarray(np.asarray(a, dtype=np.float32))
    x_prompt = f32(inp["x_prompt"]); x_sample = f32(inp["x_sample"])
    vl = []
    for i in range(4):
        vl.append(_fm(inp["norm_ffn1"][i]))
    for i in range(4):
        vl.append(_fm(inp["norm_mix"][i]))
    for i in range(4):
        vl.append(_fm(inp["norm_ffn2"][i]))
    vl.append(_fm(inp["final_norm"]))
    vl.append(_fm(inp["hgrn_o_norm"][0])); vl.append(_fm(inp["hgrn_o_norm"][1]))
    vl.append(_fm(inp["gla_o_norm"][0]))
    gb = np.zeros((128, 16), np.float32)
    gb[:, 0:8] = np.asarray(inp["gla_b_gate"][0], np.float32).reshape(8, 128).T
    vl.append(gb)
    vl.append(_fm(inp["pool_scale"][0]))
    for i in range(4):
        vl.append(_fm(inp["hgrn_lb_logits"][i]))
    vecs = np.ascontiguousarray(np.stack(vl, axis=1).reshape(128, NV * 16))
    wnames = ["ffn1_w_gate", "ffn1_w_up", "ffn1_w_down", "ffn2_w_gate", "ffn2_w_up", "ffn2_w_down",
              "hgrn_w_in", "hgrn_w_out", "gla_w_in", "gla_w_gate_up", "gla_w_out", "pool_w_group"]
    wts = {n: f32(inp[n]) for n in wnames}
    st_h = f32(inp["state_hgrn"]); st_g = f32(inp["state_gla"]); st_p = f32(inp["state_pool"])
    in_maps = []
    for c in range(NCORES):
        bq, half = c // 2, c % 2
        xin = np.concatenate([x_prompt[bq, half * 1024:(half + 1) * 1024], x_sample[c * 16:(c + 1) * 16].reshape(128, D)], 0)
        m = {"x_in": np.ascontiguousarray(xin), "st_h": np.ascontiguousarray(st_h[:, c * 16:(c + 1) * 16]),
             "st_g": np.ascontiguousarray(st_g[:, c * 16:(c + 1) * 16]),
             "st_p": np.ascontiguousarray(st_p[:, c * 16:(c + 1) * 16]),
             "cst": _consts(c), "vecs": vecs}
        m.update(wts)
        in_maps.append(m)
    if "nc" not in _NC_CACHE:
        _NC_CACHE["nc"] = build_program()
    res = run_bass_kernel_spmd(_NC_CACHE["nc"], in_maps, core_ids=list(range(NCORES)))
    R = res.results
    y_prompt = np.zeros((4, 2048, D), np.float32)
    y_sample = np.zeros((128, 8, D), np.float32)
    hp = np.zeros((2, 4, 16, 128, 128), np.float32)
    gp = np.zeros((1, 4, 4, 256, 512), np.float32)
    pp = np.zeros((1, 4, 15, D), np.float32)
    hs = np.zeros((2, 128, 16, 128, 128), np.float32)
    gs = np.zeros((1, 128, 4, 256, 512), np.float32)
    pss = np.zeros((1, 128, 15, D), np.float32)
    for c in range(NCORES):
        bq, half = c // 2, c % 2
        r = R[c]
        y_prompt[bq, half * 1024:(half + 1) * 1024] = r["y"][0:1024]
        y_sample[c * 16:(c + 1) * 16] = r["y"][1024:1152].reshape(16, 8, D)
        if half == 1:
            hp[:, bq] = r["o_hp"]
            gp[:, bq] = r["o_gp"]
            pp[:, bq] = r["o_pp"]
        hs[:, c * 16:(c + 1) * 16] = r["o_hs"]
        gs[:, c * 16:(c + 1) * 16] = r["o_gs"]
        pss[:, c * 16:(c + 1) * 16] = r["o_ps"]
    return (y_prompt, y_sample, hp, gp, pp, hs, gs, pss)
```

```python
import numpy as np
from contextlib import ExitStack
import concourse.bass as bass
import concourse.mybir as mybir
from concourse.bass_utils import run_bass_kernel_spmd

F32 = mybir.dt.float32
BF16 = mybir.dt.bfloat16
AF = mybir.ActivationFunctionType
ALU = mybir.AluOpType

D = 2048
KC = 16
NT = 1152
DFF = 5632
BLKS = [(0, 512), (512, 512), (1024, 128)]
NPART = 11
EPS = 1e-6
NCORES = 8
RG = [[0, 1], [2, 3], [4, 5], [6, 7]]
DBG_HEADS = None

C_ID = 0
C_CM32 = 128
C_CM8 = 256
C_CM128 = 384
C_SM32 = 512
C_SM8 = 516
C_SM128 = 532
C_SEG32 = 533
C_SEG8 = 1045
C_SEG128 = 1173
C_ONES = 1685
C_FLAG = 1813
C_EPS = 1814
C_INVC = 1815
C_INVW = 1879
C_END = 1883

V_NF1, V_NMIX, V_NF2, V_FIN, V_HON, V_GON, V_GB, V_PSC, V_LBL = 0, 4, 8, 12, 13, 15, 16, 17, 18
NV = 22


class Bld:
    def __init__(self, nc):
        self.nc = nc
        self.names = ['pe', 'act', 'dve', 'pool', 'sp']
        self.q = {n: [] for n in self.names}
        self.cnt = {n: 0 for n in self.names}
        self.sem = {n: nc.alloc_semaphore(name="s_" + n) for n in self.names}
        self.waited = {n: {} for n in self.names}
        self.lw = {}
        self.rd = {}
        self.nd = 24
        self.dsem = [nc.alloc_semaphore(name="d%d" % i) for i in range(self.nd)]
        self.dcnt = [0] * self.nd
        self.dnext = {'sp': 0, 'pool': 0}
        self.dbase = {'sp': 0, 'pool': 12}

    def _wait(self, eng, tok):
        sid, sem, val, src = tok
        if eng == 'pe' and src == 'pe':
            return
        if self.waited[eng].get(sid, 0) < val:
            self.waited[eng][sid] = val
            self.q[eng].append(('w', sem, val))

    def _deps(self, eng, r, w):
        for k in r:
            for t in self.lw.get(k, ()):
                self._wait(eng, t)
        for k in w:
            for t in self.lw.get(k, ()):
                self._wait(eng, t)
            for t in self.rd.get(k, {}).values():
                self._wait(eng, t)

    def _commit(self, tok, r, w):
        for k in w:
            self.lw[k] = [tok]
            self.rd[k] = {}
        for k in r:
            if k in w:
                continue
            d = self.rd.setdefault(k, {})
            o = d.get(tok[0])
            if o is None or o[2] < tok[2]:
                d[tok[0]] = tok

    def op(self, eng, fn, r=(), w=(), inc=True):
        self._deps(eng, r, w)
        val = self.cnt[eng] + 1
        tok = (eng, self.sem[eng], val, eng)
        if inc:
            self.cnt[eng] = val
            self.q[eng].append(('o', fn, self.sem[eng], 1))
        else:
            self.q[eng].append(('o', fn, None, 0))
        self._commit(tok, r, w)

    def dma(self, qn, out, in_, r=(), w=()):
        i = self.dbase[qn] + self.dnext[qn]
        self.dnext[qn] = (self.dnext[qn] + 1) % 12
        self._deps(qn, r, w)
        sid = 'd%d' % i
        if self.dcnt[i] > 0:
            self._wait(qn, (sid, self.dsem[i], self.dcnt[i] * 16, 'dma'))
        self.dcnt[i] += 1
        tok = (sid, self.dsem[i], self.dcnt[i] * 16, 'dma')
        self.q[qn].append(('o', (lambda e, o=out, s=in_: e.dma_start(out=o, in_=s)), self.dsem[i], 16))
        self._commit(tok, r, w)

    def fence(self, src, dst):
        toks = []
        for k in src:
            toks += list(self.lw.get(k, ()))
            toks += list(self.rd.get(k, {}).values())
        for k in dst:
            self.lw[k] = list(self.lw.get(k, ())) + toks

    def finish(self):
        for n in self.names:
            if self.cnt[n] > 0:
                self._wait('sp', (n, self.sem[n], self.cnt[n], n))
        for i in range(self.nd):
            if self.dcnt[i] > 0:
                self._wait('sp', ('d%d' % i, self.dsem[i], self.dcnt[i] * 16, 'dma'))

    def replay(self, eng_name, e):
        for it in self.q[eng_name]:
            if it[0] == 'w':
                e.wait_ge(it[1], it[2])
            else:
                ins = it[1](e)
                if it[2] is not None:
                    ins.then_inc(it[2], it[3])


def build_program(stages=None, nlay=4, na=2):
    nc = bass.Bass("TRN2", target_bir_lowering=False)
    es = ExitStack()

    def din(name, shape):
        return nc.dram_tensor(name, list(shape), F32, kind="ExternalInput").ap()

    def dout(name, shape):
        return nc.dram_tensor(name, list(shape), F32, kind="ExternalOutput").ap()

    x_in = din("x_in", [NT, D])
    st_h = din("st_h", [2, 16, 16, 128, 128])
    st_g = din("st_g", [1, 16, 4, 256, 512])
    st_p = din("st_p", [1, 16, 15, D])
    cst_d = din("cst", [128, C_END])
    vecs_d = din("vecs", [128, NV * 16])
    WSH = {"ffn1_w_gate": [nlay, D, DFF], "ffn1_w_up": [nlay, D, DFF], "ffn1_w_down": [nlay, DFF, D],
           "ffn2_w_gate": [nlay, D, DFF], "ffn2_w_up": [nlay, D, DFF], "ffn2_w_down": [nlay, DFF, D],
           "hgrn_w_in": [na, D, 4 * D], "hgrn_w_out": [na, D, D],
           "gla_w_in": [1, D, 6160], "gla_w_gate_up": [1, 16, 1024], "gla_w_out": [1, D, D],
           "pool_w_group": [1, 4, 512, 512]}

    class LazyW(dict):
        def __missing__(self, k):
            v = din(k, WSH[k])
            self[k] = v
            return v
    W = LazyW()

    y_out = dout("y", [NT, D])
    o_hp = dout("o_hp", [2, 16, 128, 128])
    o_gp = dout("o_gp", [1, 4, 256, 512])
    o_pp = dout("o_pp", [1, 15, D])
    o_hs = dout("o_hs", [2, 16, 16, 128, 128])
    o_gs = dout("o_gs", [1, 16, 4, 256, 512])
    o_ps = dout("o_ps", [1, 16, 15, D])

    og_d = nc.dram_tensor("og_d", [128, 16, NT], BF16).ap()
    cc_in = [nc.dram_tensor("cc_in%d" % i, [2048, 128], F32) for i in range(2)]
    cc_out = [nc.dram_tensor("cc_out%d" % i, [4096, 128], F32) for i in range(2)]
    ccg_in = nc.dram_tensor("ccg_in", [1024, 512], F32)
    ccg_out = nc.dram_tensor("ccg_out", [2048, 512], F32)
    ccp_in = nc.dram_tensor("ccp_in", [128, 16 * 15], F32)
    ccp_out = nc.dram_tensor("ccp_out", [256, 16 * 15], F32)

    def sb(name, shape, dt):
        return es.enter_context(nc.sbuf_tensor(name, list(shape), dt))

    xs = sb("xs", [128, KC, NT], F32)
    xb = sb("xb", [128, KC, NT], BF16)
    BIGN = 33792 - 2048
    big = sb("big", [128, BIGN], BF16)
    fa = sb("fa", [128, 2048], F32)
    osq = sb("osq", [128, NT], F32)
    rsa = sb("rsa", [128, NT], F32)
    Sf = [sb("Sf%d" % i, [128, 2, 512], F32) for i in range(2)]
    cst = sb("cst_sb", [128, C_END], F32)
    vecs = sb("vecs_sb", [128, NV, 16], F32)
    lbt = sb("lbt", [128, 4, 16], F32)
    ebl = sb("ebl", [128, 2, 16], F32)
    tsum = sb("tsum", [128, 2, 2], F32)
    SQB = sb("sqb", [128, 512], BF16)[:, :]
    ONB = sb("onb", [128, 128], BF16)[:, :]
    ps = [es.enter_context(nc.psum_tensor("ps%d" % i, [128, 512], F32)) for i in range(8)]

    b = Bld(nc)
    T = [fa[:, i * 512:(i + 1) * 512] for i in range(4)]
    TK = ['T0', 'T1', 'T2', 'T3']

    def bslice(off, n):
        return big[:, off:off + n]
    HB = [bslice(0, 4608), bslice(4608, 4608)]
    WD = [bslice(9216 + i * 2048, 2048) for i in range(7)]
    WGU = [bslice(9216 + 14336 + i * 2048, 2048) for i in range(4)]
    FFN_KEYS = ['hb0', 'hb1'] + ['wd%d' % i for i in range(7)] + ['wgu%d' % i for i in range(4)]
    WR = [bslice(i * 2048, 2048) for i in range(6)]
    mo = 6 * 2048
    QE = bslice(mo, 1024); mo += 1024
    KD = bslice(mo, 1024); mo += 1024
    K2T = bslice(mo, 1024); mo += 1024
    VT = [bslice(mo + i * 512, 512) for i in range(2)]; mo += 1024
    ATT = bslice(mo, 128); mo += 128
    K2M = [bslice(mo + i * 256, 256) for i in range(4)]; mo += 1024
    SB_ = [bslice(mo + i * 1024, 1024) for i in range(2)]; mo += 2048
    GLB = bslice(mo, 512); mo += 512
    OGH = bslice(mo, 4608); mo += 4608
    PB = bslice(mo, 3840); mo += 3840
    WGUB = bslice(mo, 1024); mo += 1024
    YP = OGH
    assert mo <= BIGN, mo
    MIX_KEYS = ['wr%d' % i for i in range(6)] + ['qe', 'kd', 'k2t', 'vt0', 'vt1', 'att', 'k2m0', 'k2m1', 'k2m2', 'k2m3',
                                                'sb0', 'sb1', 'glb', 'ogh', 'wgub', 'yp', 'pb', 'vta'] + ['k2m8_%d' % i for i in range(8)] + ['sb0_%d' % i for i in range(8)]

    cI = cst[:, C_ID:C_ID + 128]
    ones_f = cst[:, C_ONES:C_ONES + 128]
    flag = cst[:, C_FLAG:C_FLAG + 1]
    epsc = cst[:, C_EPS:C_EPS + 1]

    bank = [0]

    held = set()

    def nb(excl=()):
        while True:
            i = bank[0]
            bank[0] = (i + 1) % 8
            if i not in excl and i not in held:
                return i

    def mm(out, lhsT, rhs, start, stop, r, w, inc):
        b.op('pe', lambda e: e.matmul(out, lhsT, rhs, start=start, stop=stop), r=r, w=w, inc=inc)

    def tr(out, in_, r, w):
        b.op('pe', lambda e: e.transpose(out, in_, cI), r=r + ['cst'], w=w)

    def act(out, in_, func, r, w, scale=None, bias=None):
        kw = {}
        if scale is not None:
            kw['scale'] = scale
        if bias is not None:
            kw['bias'] = bias
        b.op('act', lambda e: e.activation(out, in_, func, **kw), r=r, w=w)

    def tt(out, in0, in1, op, r, w):
        b.op('dve', lambda e: e.tensor_tensor(out, in0, in1, op), r=r, w=w)

    def ts(out, in0, s1, s2, op0, op1, r, w):
        if op1 is None:
            b.op('dve', lambda e: e.tensor_scalar(out, in0, s1, None, op0), r=r, w=w)
        else:
            b.op('dve', lambda e: e.tensor_scalar(out, in0, s1, s2, op0, op1), r=r, w=w)

    def stt(out, in0, sc, in1, op0, op1, r, w):
        b.op('dve', lambda e: e.scalar_tensor_tensor(out, in0, sc, in1, op0, op1), r=r, w=w)

    def cp(eng, out, in_, r, w):
        if eng == 'act':
            b.op('act', lambda e: e.activation(out, in_, AF.Copy), r=r, w=w)
        else:
            b.op('dve', lambda e: e.tensor_copy(out, in_), r=r, w=w)

    def wtile(slot):
        return slot.rearrange("p (kc n) -> p kc n", n=128)

    def load_coltile(slot, key, wsrc2d, c0, ncols=128):
        src = wsrc2d.rearrange("(kc p) n -> p kc n", p=128)[:, :, c0:c0 + ncols]
        kcn = wsrc2d.shape[0] // 128
        dst = slot[:, 0:kcn * ncols].rearrange("p (kc n) -> p kc n", n=ncols)
        b.dma('pool', dst, src, r=[], w=[key])
        return dst

    b.dma('sp', cst[:, :], cst_d[:, :], w=['cst'])
    b.dma('sp', vecs[:, :, :], vecs_d.rearrange("p (v k) -> p v k", k=16), w=['vecs'])

    def vec(i):
        return vecs[:, i, :]

    def lower_bounds():
        L = [vec(V_LBL + i) for i in range(4)]
        m = T[0][:, 0:16]; e = [T[1][:, i * 16:(i + 1) * 16] for i in range(4)]; s = T[0][:, 16:32]
        tt(m, L[0], L[1], ALU.max, r=['vecs'], w=['T0'])
        tt(m, m, L[2], ALU.max, r=['vecs', 'T0'], w=['T0'])
        tt(m, m, L[3], ALU.max, r=['vecs', 'T0'], w=['T0'])
        for i in range(4):
            tt(e[i], L[i], m, ALU.subtract, r=['vecs', 'T0'], w=['T1'])
            act(e[i], e[i], AF.Exp, r=['T1'], w=['T1'])
        tt(s, e[0], e[1], ALU.add, r=['T1'], w=['T0'])
        tt(s, s, e[2], ALU.add, r=['T1', 'T0'], w=['T0'])
        tt(s, s, e[3], ALU.add, r=['T1', 'T0'], w=['T0'])
        b.op('dve', lambda en: en.reciprocal(s, s), r=['T0'], w=['T0'])
        tt(lbt[:, 3, :], e[0], s, ALU.mult, r=['T0', 'T1'], w=['lbt'])
        ts(lbt[:, 2, :], lbt[:, 3, :], -1.0, 1.0, ALU.mult, ALU.add, r=['lbt'], w=['lbt'])
        b.op('dve', lambda en: en.memset(lbt[:, 0, :], 0.0), w=['lbt'])
        b.op('dve', lambda en: en.memset(lbt[:, 1, :], 1.0), w=['lbt'])

    def load_x():
        for tt_ in range(9):
            b.dma('sp', fa[:, :], x_in[tt_ * 128:(tt_ + 1) * 128, :], w=TK)
            for g4 in range(4):
                bk = nb()
                for i in range(4):
                    kc = g4 * 4 + i
                    tr(ps[bk][:, i * 128:(i + 1) * 128], fa[:, kc * 128:(kc + 1) * 128], r=TK, w=['ps%d' % bk])
                cp('act' if g4 % 2 else 'dve', xs[:, g4 * 4:(g4 + 1) * 4, tt_ * 128:(tt_ + 1) * 128],
                   ps[bk][:, :].rearrange("p (a n) -> p a n", n=128), r=['ps%d' % bk], w=['xs'])

    def rms_rstd(blk, dst, dkey, src_fn, src_keys, nchunk, scale_n):
        off, L = blk
        bk = nb()
        for kc in range(nchunk):
            act(SQB[:, 0:L], src_fn(kc), AF.Square, r=src_keys, w=['sqb'])
            mm(ps[bk][:, 0:L], ONB, SQB[:, 0:L], kc == 0, kc == nchunk - 1, r=['sqb', 'onb'], w=['ps%d' % bk],
               inc=True)
        act(dst, ps[bk][:, 0:L], AF.Ln, r=['ps%d' % bk, 'cst'], w=[dkey], scale=1.0 / scale_n, bias=epsc)
        act(dst, dst, AF.Exp, r=[dkey], w=[dkey], scale=-0.5)

    def norm_to_xb(gv):
        for blk in BLKS:
            off, L = blk
            rms_rstd(blk, T[0][:, 0:L], 'T0', lambda kc: xs[:, kc, off:off + L], ['xs'], KC, float(D))
            for kc in range(KC):
                stt(xb[:, kc, off:off + L], xs[:, kc, off:off + L], vec(gv)[:, kc:kc + 1], T[0][:, 0:L],
                    ALU.mult, ALU.mult, r=['xs', 'vecs', 'T0'], w=['xb'])

    def ffn(li, which):
        wg = W["ffn%d_w_gate" % which][li]
        wu = W["ffn%d_w_up" % which][li]
        wd = W["ffn%d_w_down" % which][li]
        norm_to_xb((V_NF1 if which == 1 else V_NF2) + li)
        b.fence(MIX_KEYS, FFN_KEYS)
        gi = [0]
        di = [0]
        for part in range(NPART):
            hb = HB[part % 2]
            hk = 'hb%d' % (part % 2)
            hv = hb.rearrange("p (f t) -> p f t", t=NT)
            wds = []
            for fi in range(4):
                f = part * 4 + fi
                tiles = []
                for wsrc in (wg, wu):
                    s = gi[0] % 4
                    gi[0] += 1
                    tiles.append((load_coltile(WGU[s], 'wgu%d' % s, wsrc, f * 128), 'wgu%d' % s))
                s = di[0] % 7
                di[0] += 1
                b.dma('pool', WD[s], wd[f * 128:(f + 1) * 128, :], w=['wd%d' % s])
                wds.append((WD[s], 'wd%d' % s))
                banks = []
                for m in range(2):
                    wt, wk = tiles[m]
                    bks = [nb() for _ in range(3)]
                    banks.append(bks)
                    for kc in range(KC):
                        for bi, (off, L) in enumerate(BLKS):
                            mm(ps[bks[bi]][:, 0:L], wt[:, kc, :], xb[:, kc, off:off + L], kc == 0, kc == KC - 1,
                               r=[wk, 'xb'], w=['ps%d' % bks[bi]], inc=(kc == KC - 1))
                for bi, (off, L) in enumerate(BLKS):
                    tkey = TK[1 + (bi % 2)]
                    tmp = T[1 + (bi % 2)][:, 0:L]
                    act(tmp, ps[banks[0][bi]][:, 0:L], AF.Silu, r=['ps%d' % banks[0][bi]], w=[tkey])
                    tt(hv[:, fi, off:off + L], tmp, ps[banks[1][bi]][:, 0:L], ALU.mult,
                       r=[tkey, 'ps%d' % banks[1][bi]], w=[hk])
            for oc in range(KC):
                for bi, (off, L) in enumerate(BLKS):
                    bk = nb()
                    for fi in range(4):
                        wt, wk = wds[fi]
                        mm(ps[bk][:, 0:L], wt[:, oc * 128:(oc + 1) * 128], hv[:, fi, off:off + L], fi == 0, fi == 3,
                           r=[wk, hk], w=['ps%d' % bk], inc=(fi == 3))
                    stt(xs[:, oc, off:off + L], ps[bk][:, 0:L], 0.5, xs[:, oc, off:off + L], ALU.mult, ALU.add,
                        r=['ps%d' % bk, 'xs'], w=['xs'])
        b.fence(FFN_KEYS, MIX_KEYS)

    def rec_tile(nkc, nvc, tl, tile_cols, vt, vtk, subs, cmask, smask_col0, ebl_col0, state_only,
                 get_state, put_state, o_sink):
        V = nvc * 128
        c0 = tile_cols
        qe3 = QE[:, 0:nkc * 512].rearrange("p (k t) -> p k t", t=512)
        kd3 = KD[:, 0:nkc * 512].rearrange("p (k t) -> p k t", t=512)
        k2t3 = K2T.rearrange("p (a n) -> p a n", n=256)
        nsub, slen = subs
        bo = None
        if not state_only:
            ba = nb()
            for kc in range(nkc):
                mm(ps[ba][:, 0:128], kd3[:, kc, c0:c0 + 128], qe3[:, kc, c0:c0 + 128], kc == 0, kc == nkc - 1,
                   r=['kd', 'qe'], w=['ps%d' % ba], inc=(kc == nkc - 1))
            tt(ATT, ps[ba][:, 0:128], cmask, ALU.mult, r=['ps%d' % ba, 'cst'], w=['att'])
            bo = []
            for _ in range(nvc):
                bo.append(nb(excl=bo))
            for vc in range(nvc):
                mm(ps[bo[vc]][:, 0:128], vt[:, vc * 128:(vc + 1) * 128], ATT, True, False,
                   r=[vtk, 'att'], w=['ps%d' % bo[vc]], inc=True)
        for j in range(nsub):
            S, Sk, Sb, Sbk = get_state(j)
            cj = c0 + j * slen
            if not state_only:
                for vc in range(nvc):
                    for kc in range(nkc):
                        last = (j == nsub - 1) and (kc == nkc - 1)
                        mm(ps[bo[vc]][:, j * slen:(j + 1) * slen],
                           Sb[:, kc * V + vc * 128: kc * V + (vc + 1) * 128],
                           qe3[:, kc, cj:cj + slen], False, last,
                           r=[Sbk, 'qe'], w=['ps%d' % bo[vc]], inc=True)
            for kc in range(nkc):
                m = K2M[(j * nkc + kc) % 4]
                mk = 'k2m%d' % ((j * nkc + kc) % 4)
                ts(m[:, 0:128], k2t3[:, tl, kc * 128:(kc + 1) * 128], cst[:, smask_col0 + j:smask_col0 + j + 1], None,
                   ALU.mult, None, r=['k2t', 'cst'], w=[mk])
                bs = nb(excl=(bo or ()))
                mm(ps[bs][:, 0:V], m[:, 0:128], vt[:, 0:V], True, True, r=[mk, vtk], w=['ps%d' % bs], inc=True)
                S2, S2k = put_state(j)
                stt(S2[:, kc, 0:V], S[:, kc, 0:V], ebl[:, kc, ebl_col0 + j:ebl_col0 + j + 1], ps[bs][:, 0:V],
                    ALU.mult, ALU.add, r=[Sk, 'ebl', 'ps%d' % bs], w=[S2k])
            put_state(j, done=True)
        if not state_only:
            for vc in range(nvc):
                o_sink(vc, ps[bo[vc]][:, 0:128], 'ps%d' % bo[vc])
        return bo or []

    K2M8 = [K2M[i // 2][:, (i % 2) * 128:(i % 2) * 128 + 128] for i in range(8)]

    def rec_tile_prompt(nkc, nvc, tl, c0, vt, vtk, nsub, slen, cmask, smask_col0, ebl_col0, Spp, sbslot, g0, o_sink,
                        staged=False):
        V = nvc * 128
        qe3 = QE[:, 0:nkc * 512].rearrange("p (k t) -> p k t", t=512)
        kd3 = KD[:, 0:nkc * 512].rearrange("p (k t) -> p k t", t=512)
        k2t3 = K2T.rearrange("p (a n) -> p a n", n=256)
        cap = 512 // V
        nreg = nsub * nkc
        st = {}

        def stage_a():
            bd = []
            for _ in range((nreg + cap - 1) // cap):
                bd.append(nb(excl=bd))
            reg = []
            for j in range(nsub):
                for kc in range(nkc):
                    idx = j * nkc + kc
                    if nsub > 1:
                        mi = (tl * nreg + idx) % 8
                        m, mk = K2M8[mi], 'k2m8_%d' % mi
                        ts(m, k2t3[:, tl, kc * 128:(kc + 1) * 128], cst[:, smask_col0 + j:smask_col0 + j + 1],
                           None, ALU.mult, None, r=['k2t', 'cst'], w=[mk])
                        lhs, lk = m, mk
                    else:
                        lhs, lk = k2t3[:, tl, kc * 128:(kc + 1) * 128], 'k2t'
                    bk_ = bd[idx // cap]
                    ra = ps[bk_][:, (idx % cap) * V:(idx % cap + 1) * V]
                    mm(ra, lhs, vt[:, 0:V], True, True, r=[lk, vtk], w=['ps%d' % bk_], inc=True)
                    reg.append((ra, 'ps%d' % bk_))
            st['bd'] = bd
            st['reg'] = reg
            for x_ in bd:
                held.add(x_)

        def stage_b():
            reg = st['reg']
            for j in range(nsub):
                Si, Sik = Spp[(g0 + j) % 2]
                So, Sok = Spp[(g0 + j + 1) % 2]
                for kc in range(nkc):
                    ra, rk = reg[j * nkc + kc]
                    stt(So[:, kc, 0:V], Si[:, kc, 0:V], ebl[:, kc, ebl_col0 + j:ebl_col0 + j + 1], ra,
                        ALU.mult, ALU.add, r=[Sik, 'ebl', rk], w=[Sok])
                sl, slk = sbslot(g0 + j + 1)
                if nkc == 1:
                    cp('act', sl[:, 0:V], So[:, 0, 0:V], r=[Sok], w=[slk])
                else:
                    cp('act', sl[:, 0:nkc * V], So[:, :, :].rearrange("p a v -> p (a v)"), r=[Sok], w=[slk])
            for x_ in st['bd']:
                held.discard(x_)

        def stage_c():
            ba = nb()
            for kc in range(nkc):
                mm(ps[ba][:, 0:128], kd3[:, kc, c0:c0 + 128], qe3[:, kc, c0:c0 + 128], kc == 0, kc == nkc - 1,
                   r=['kd', 'qe'], w=['ps%d' % ba], inc=(kc == nkc - 1))
            tt(ATT, ps[ba][:, 0:128], cmask, ALU.mult, r=['ps%d' % ba, 'cst'], w=['att'])
            bo = []
            for _ in range(nvc):
                bo.append(nb(excl=bo))
            for vc in range(nvc):
                mm(ps[bo[vc]][:, 0:128], vt[:, vc * 128:(vc + 1) * 128], ATT, True, False,
                   r=[vtk, 'att'], w=['ps%d' % bo[vc]], inc=True)
            for j in range(nsub):
                sl, slk = sbslot(g0 + j)
                cj = c0 + j * slen
                for vc in range(nvc):
                    for kc in range(nkc):
                        last = (j == nsub - 1) and (kc == nkc - 1)
                        mm(ps[bo[vc]][:, j * slen:(j + 1) * slen], sl[:, kc * V + vc * 128: kc * V + (vc + 1) * 128],
                           qe3[:, kc, cj:cj + slen], False, last, r=[slk, 'qe'], w=['ps%d' % bo[vc]], inc=True)
            st['bo'] = bo
            for x_ in bo:
                held.add(x_)
            for vc in range(nvc):
                o_sink(vc, ps[bo[vc]][:, 0:128], 'ps%d' % bo[vc])

        def release():
            for x_ in st.get('bo', []):
                held.discard(x_)
        if staged:
            return stage_a, stage_b, stage_c, release, st
        stage_a()
        stage_b()
        stage_c()
        release()
        return st['bo']

    def pass1_decay(kc_i, blk_i, L, kk_ap, kk_keys, lf_scale):
        b.op('dve', lambda e: e.tensor_tensor_scan(T[3][:, 0:L], cst[:, C_ONES:C_ONES + 1].broadcast_to([128, L]),
                                                   T[2][:, 0:L], 0.0, ALU.mult, ALU.add), r=['cst', 'T2'], w=['T3'])
        if blk_i == 1:
            cp('dve', tsum[:, kc_i, 0:1], T[3][:, L - 1:L], r=['T3'], w=['tsum'])
            sc_ = tsum[:, kc_i, 0:1]
        else:
            tt(tsum[:, kc_i, 1:2], T[3][:, L - 1:L], tsum[:, kc_i, 0:1], ALU.add, r=['T3', 'tsum'], w=['tsum'])
            sc_ = tsum[:, kc_i, 1:2]
        ts(T[3][:, 0:L], T[3][:, 0:L], sc_, None, ALU.subtract, None, r=['T3', 'tsum'], w=['T3'])
        act(T[3][:, 0:L], T[3][:, 0:L], AF.Exp, r=['T3'], w=['T3'], scale=-lf_scale)
        tt(T[1][:, 0:L], kk_ap, T[3][:, 0:L], ALU.mult, r=kk_keys + ['T3'], w=['T1'])

    def decay_chain(nkc_i, L, seg_col, slen, kk_ap, kk_keys, lf_scale):
        nseg = L // slen
        kd3 = KD[:, 0:2 * 512].rearrange("p (k t) -> p k t", t=512)
        b.op('dve', lambda e: e.tensor_tensor_scan(T[3][:, 0:L], cst[:, seg_col:seg_col + L], T[2][:, 0:L], 0.0,
                                                   ALU.mult, ALU.add), r=['cst', 'T2'], w=['T3'])
        act(T[2][:, 0:L], T[3][:, 0:L], AF.Exp, r=['T3'], w=['T2'], scale=lf_scale)
        act(T[3][:, 0:L], T[3][:, 0:L], AF.Exp, r=['T3'], w=['T3'], scale=-lf_scale)
        tt(T[3][:, 0:L], kk_ap, T[3][:, 0:L], ALU.mult, r=kk_keys + ['T3'], w=['T3'])
        cp('act', kd3[:, nkc_i, 0:L], T[3][:, 0:L], r=['T3'], w=['kd'])
        ebv = T[2][:, 0:L].rearrange("p (s l) -> p s l", l=slen)
        cp('dve', ebl[:, nkc_i, 0:nseg], ebv[:, :, slen - 1], r=['T2'], w=['ebl'])
        tt(T[1][:, 0:L].rearrange("p (s l) -> p s l", l=slen), T[3][:, 0:L].rearrange("p (s l) -> p s l", l=slen),
           ebv[:, :, slen - 1:slen].broadcast_to([128, nseg, slen]), ALU.mult, r=['T3', 'T2'], w=['T1'])

    def k2_transposes(nkc_i, L):
        k2t3 = K2T.rearrange("p (a n) -> p a n", n=256)
        nt_ = L // 128
        bk = nb()
        for t_ in range(nt_):
            tr(ps[bk][:, t_ * 128:(t_ + 1) * 128], T[1][:, t_ * 128:(t_ + 1) * 128], r=['T1'], w=['ps%d' % bk])
        cp('act', k2t3[:, 0:nt_, nkc_i * 128:(nkc_i + 1) * 128],
           ps[bk][:, 0:nt_ * 128].rearrange("p (a n) -> p a n", n=128), r=['ps%d' % bk], w=['k2t'])

    def proj(dst_bank, wt, wk, off, L):
        for kc in range(KC):
            mm(ps[dst_bank][:, 0:L], wt[:, kc, :], xb[:, kc, off:off + L], kc == 0, kc == KC - 1,
               r=[wk, 'xb'], w=['ps%d' % dst_bank], inc=(kc == KC - 1))

    wri = [0]

    def next_w(wsrc2d, c0):
        s = wri[0] % 6
        wri[0] += 1
        return load_coltile(WR[s], 'wr%d' % s, wsrc2d, c0), 'wr%d' % s

    def vtok(wt, wk, off_tok, nvc, vi):
        bk = nb()
        for vc in range(nvc):
            t_, k_ = wt[vc]
            for kc in range(KC):
                mm(ps[bk][:, vc * 128:(vc + 1) * 128], xb[:, kc, off_tok:off_tok + 128], t_[:, kc, :], kc == 0,
                   kc == KC - 1, r=[k_, 'xb'], w=['ps%d' % bk], inc=(kc == KC - 1))
        cp('act', VT[vi][:, 0:nvc * 128], ps[bk][:, 0:nvc * 128], r=['ps%d' % bk], w=['vt%d' % vi])
        return VT[vi], 'vt%d' % vi

    VTALL3 = PB[:, 0:2048].rearrange("p (a v) -> p a v", v=512)

    def vtok_block(wlist, off, L, nvc, pre=None):
        nt_ = L // 128
        for vc in range(nvc):
            if pre is not None:
                bv = pre[vc]
                held.discard(bv)
            else:
                wt, wk = wlist[vc]
                bv = nb()
                proj(bv, wt, wk, off, L)
            ti = 1 if vc % 2 == 0 else 3
            cp('act', T[ti][:, 0:L], ps[bv][:, 0:L], r=['ps%d' % bv], w=[TK[ti]])
            bt = nb()
            for t_ in range(nt_):
                tr(ps[bt][:, t_ * 128:(t_ + 1) * 128], T[ti][:, t_ * 128:(t_ + 1) * 128], r=[TK[ti]], w=['ps%d' % bt])
            cp('dve', VTALL3[:, 0:nt_, vc * 128:(vc + 1) * 128],
               ps[bt][:, 0:nt_ * 128].rearrange("p (a n) -> p a n", n=128), r=['ps%d' % bt], w=['vta'])

        def vt_of(tl):
            return VTALL3[:, tl, 0:nvc * 128], 'vta'
        return vt_of

    def hgrn(li, j):
        win = W["hgrn_w_in"][j]
        wout = W["hgrn_w_out"][j]
        lb_i = 0 if li == 0 else 2
        norm_to_xb(V_NMIX + li)
        b.op('dve', lambda e: e.memset(osq[:, :], 0.0), w=['osq'])
        ogh3 = OGH.rearrange("p (a t) -> p a t", t=NT)
        S0 = Sf[0]
        cci = cc_in[j].ap()
        cco = cc_out[j].ap()

        k2t3_ = K2T.rearrange("p (a n) -> p a n", n=256)

        def sbslot(g):
            return SB_[0][:, (g % 8) * 128:(g % 8) * 128 + 128], 'sb0_%d' % (g % 8)
        Spp = [(Sf[0][:, 0:1, 0:128], 'Sf0'), (Sf[0][:, 1:2, 0:128], 'Sf0b')]

        def fchain(h, off, L):
            wf, wfk = next_w(win, D + h * 128)
            bf_ = nb()
            proj(bf_, wf, wfk, off, L)
            act(T[0][:, 0:L], ps[bf_][:, 0:L], AF.Sigmoid, r=['ps%d' % bf_], w=['T0'])
            ts(T[0][:, 0:L], T[0][:, 0:L], lbt[:, lb_i + 1, h:h + 1], lbt[:, lb_i, h:h + 1], ALU.mult, ALU.add,
               r=['T0', 'lbt'], w=['T0'])
            act(T[2][:, 0:L], T[0][:, 0:L], AF.Ln, r=['T0'], w=['T2'])
            ts(T[0][:, 0:L], T[0][:, 0:L], -1.0, 1.0, ALU.mult, ALU.add, r=['T0'], w=['T0'])

        def head_p1(h):
            bS = nb()
            held.add(bS)
            first = True
            for bi in (1, 0):
                off, L = BLKS[bi]
                fchain(h, off, L)
                wv_, wvk_ = next_w(win, 2 * D + h * 128)
                bv_ = nb()
                proj(bv_, wv_, wvk_, off, L)
                held.add(bv_)
                pass1_decay(0, bi, L, T[0][:, 0:L], ['T0'], 1.0)
                k2_transposes(0, L)
                vt_of = vtok_block(None, off, L, 1, pre=[bv_])
                for tl in range(4):
                    vt, vtk = vt_of(tl)
                    last = (bi == 0 and tl == 3)
                    mm(ps[bS][:, 0:128], k2t3_[:, tl, 0:128], vt[:, 0:128], first, last, r=['k2t', vtk],
                       w=['ps%d' % bS], inc=True)
                    first = False
            cp('dve', S0[:, 0, 0:128], ps[bS][:, 0:128], r=['ps%d' % bS], w=['Sf0'])
            held.discard(bS)
            for hh in ([h] if DBG_HEADS is None else range(h, 16)):
                b.dma('sp', cci[hh * 128:(hh + 1) * 128, :], S0[:, 0, 0:128], r=['Sf0'], w=['cci'])

        def head_p2(h):
            b.fence(['Sf0', 'k2m0', 'k2m1', 'k2m2', 'k2m3'], ['Sf0b'] + ['k2m8_%d' % i for i in range(8)])
            b.dma('sp', S0[:, 0, 0:128], cco[h * 128:(h + 1) * 128, :], w=['Sf0'], r=['cco'])
            ts(S0[:, 0, 0:128], S0[:, 0, 0:128], flag, None, ALU.mult, None, r=['Sf0', 'cst'], w=['Sf0'])
            sl0, sl0k = sbslot(0)
            cp('act', sl0, S0[:, 0, 0:128], r=['Sf0'], w=[sl0k])
            g = 0
            for bi, (off, L) in enumerate(BLKS):
                samp = (bi == 2)
                slen = 8 if samp else 32
                fchain(h, off, L)
                wq, wqk = next_w(win, h * 128)
                bq = nb()
                proj(bq, wq, wqk, off, L)
                held.add(bq)
                wg_, wgk = next_w(win, 3 * D + h * 128)
                bg = nb()
                proj(bg, wg_, wgk, off, L)
                held.add(bg)
                wv_, wvk_ = next_w(win, 2 * D + h * 128)
                bv_ = nb()
                proj(bv_, wv_, wvk_, off, L)
                held.add(bv_)
                decay_chain(0, L, C_SEG8 if samp else C_SEG32, slen, T[0][:, 0:L], ['T0'], 1.0)
                k2_transposes(0, L)
                act(T[0][:, 0:L], ps[bq][:, 0:L], AF.Silu, r=['ps%d' % bq], w=['T0'])
                held.discard(bq)
                stt(QE[:, 0:L], T[0][:, 0:L], float(128 ** -0.5), T[2][:, 0:L], ALU.mult, ALU.mult,
                    r=['T0', 'T2'], w=['qe'])
                act(ogh3[:, 0, off:off + L], ps[bg][:, 0:L], AF.Silu, r=['ps%d' % bg], w=['ogh'])
                held.discard(bg)
                vt_of = vtok_block(None, off, L, 1, pre=[bv_])
                if samp:
                    b.fence(['Sf0b'] + ['k2m8_%d' % i for i in range(8)], ['Sf0', 'k2m0', 'k2m1', 'k2m2', 'k2m3'])
                stages = []
                for tl in range(L // 128):
                    vt, vtk = vt_of(tl)

                    def o_sink(vc, pso, pk, tl=tl):
                        c_ = off + tl * 128
                        act(T[0][:, 0:128], pso, AF.Square, r=[pk], w=['T0'])
                        tt(osq[:, c_:c_ + 128], osq[:, c_:c_ + 128], T[0][:, 0:128], ALU.add, r=['osq', 'T0'],
                           w=['osq'])
                        stt(ogh3[:, 0, c_:c_ + 128], pso, vec(V_HON + j)[:, h:h + 1], ogh3[:, 0, c_:c_ + 128],
                            ALU.mult, ALU.mult, r=[pk, 'vecs', 'ogh'], w=['ogh'])
                    if not samp:
                        stages.append(rec_tile_prompt(1, 1, tl, tl * 128, vt, vtk, 4, 32, cst[:, C_CM32:C_CM32 + 128],
                                                      C_SM32, tl * 4, Spp, sbslot, g, o_sink, staged=True))
                        g += 4
                    else:
                        def bufs(jj):
                            if jj // 8 == 0:
                                return Sf[1], 'Sf1', SB_[1], ['sb1']
                            return Sf[0], 'Sf0', SB_[0], ['sb0_%d' % i for i in range(8)]

                        def get_state(jj):
                            S1, S1k, B1, B1k = bufs(jj)
                            s1v = S1[:, :, :].rearrange("p a (s v) -> p (a s) v", v=128)
                            if jj % 8 == 0:
                                g0 = jj
                                src = st_h[j, g0:g0 + 8, h, :, :].rearrange("s k v -> k s v")
                                b.dma('sp', s1v[:, 0:8, :], src, w=[S1k])
                                cp('act', B1[:, 0:1024].rearrange("p (s v) -> p s v", v=128), s1v[:, 0:8, :],
                                   r=[S1k], w=B1k)
                            q_ = jj % 8
                            return (S1[:, q_ // 4:q_ // 4 + 1, (q_ % 4) * 128:(q_ % 4) * 128 + 128], S1k,
                                    B1[:, q_ * 128:(q_ + 1) * 128], B1k[0])

                        def put_state(jj, done=False):
                            S1, S1k, B1, B1k = bufs(jj)
                            s1v = S1[:, :, :].rearrange("p a (s v) -> p (a s) v", v=128)
                            q_ = jj % 8
                            if done:
                                if q_ == 7:
                                    g0 = jj - 7
                                    dst = o_hs[j, g0:g0 + 8, h, :, :].rearrange("s k v -> k s v")
                                    b.dma('sp', dst, s1v[:, 0:8, :], r=[S1k])
                                return None
                            return S1[:, q_ // 4:q_ // 4 + 1, (q_ % 4) * 128:(q_ % 4) * 128 + 128], S1k
                        rec_tile(1, 1, tl, tl * 128, vt, vtk, (16, 8), cst[:, C_CM8:C_CM8 + 128], C_SM8, 0, False,
                                 get_state, put_state, o_sink)
                if not samp:
                    n_ = len(stages)
                    stages[0][0]()
                    stages[0][1]()
                    for t_ in range(n_):
                        if t_ + 1 < n_:
                            stages[t_ + 1][0]()
                        stages[t_][2]()
                        stages[t_][3]()
                        if t_ + 1 < n_:
                            stages[t_ + 1][1]()
                if bi == 1:
                    b.dma('sp', o_hp[j, h, :, :], S0[:, 0, 0:128], r=['Sf0'])
            for hh in ([h] if DBG_HEADS is None else range(h, 16)):
                b.dma('sp', og_d[:, hh, :], ogh3[:, 0, :], r=['ogh'], w=['og_d'])

        NH = 16 if DBG_HEADS is None else DBG_HEADS
        for h in range(NH):
            head_p1(h)
        b.op('pool', lambda e: e.collective_compute("AllGather", ALU.bypass,
                                                    replica_groups=RG,
                                                    ins=[cci[:, :]], outs=[cco[:, :]]), r=['cci'], w=['cco'])
        for h in range(NH):
            head_p2(h)
        for (off, L) in BLKS:
            bk = nb()
            mm(ps[bk][:, 0:L], ones_f, osq[:, off:off + L], True, True, r=['cst', 'osq'], w=['ps%d' % bk], inc=True)
            act(rsa[:, off:off + L], ps[bk][:, 0:L], AF.Ln, r=['ps%d' % bk, 'cst'], w=['rsa'], scale=1.0 / D,
                bias=epsc)
            act(rsa[:, off:off + L], rsa[:, off:off + L], AF.Exp, r=['rsa'], w=['rsa'], scale=-0.5)
        b.dma('sp', xb[:, :, :], og_d[:, :, :], r=['og_d'], w=['xb'])
        for oc in range(KC):
            wt, wk = next_w(wout, oc * 128)
            for (off, L) in BLKS:
                bk = nb()
                for rt in range(KC):
                    mm(ps[bk][:, 0:L], wt[:, rt, :], xb[:, rt, off:off + L], rt == 0, rt == KC - 1,
                       r=[wk, 'xb'], w=['ps%d' % bk], inc=(rt == KC - 1))
                tt(T[0][:, 0:L], ps[bk][:, 0:L], rsa[:, off:off + L], ALU.mult, r=['ps%d' % bk, 'rsa'], w=['T0'])
                tt(xs[:, oc, off:off + L], xs[:, oc, off:off + L], T[0][:, 0:L], ALU.add, r=['xs', 'T0'], w=['xs'])

    def gla():
        win = W["gla_w_in"][0]
        wout = W["gla_w_out"][0]
        wgu = W["gla_w_gate_up"][0]
        norm_to_xb(V_NMIX + 1)
        b.dma('pool', WGUB[0:16, 0:1024], wgu[:, :], w=['wgub'])
        ogh3 = OGH.rearrange("p (a t) -> p a t", t=NT)
        qe3 = QE[:, 0:1024].rearrange("p (k t) -> p k t", t=512)
        S0 = Sf[0]
        cci = ccg_in.ap()
        cco = ccg_out.ap()

        k2t3_ = K2T.rearrange("p (a n) -> p a n", n=256)

        def sbslot(g):
            return SB_[g % 2][:, 0:1024], 'sb%d' % (g % 2)

        def glow_proj(off, L):
            s_ = wri[0] % 6
            wri[0] += 1
            wl = load_coltile(WR[s_], 'wr%d' % s_, win, 6144, 16)
            bl = nb()
            for kc in range(KC):
                mm(ps[bl][0:16, 0:L], wl[:, kc, :], xb[:, kc, off:off + L], kc == 0, kc == KC - 1,
                   r=['wr%d' % s_, 'xb'], w=['ps%d' % bl], inc=(kc == KC - 1))
            cp('act', GLB[0:16, 0:L], ps[bl][0:16, 0:L], r=['ps%d' % bl], w=['glb'])

        def gk_chain(h, k2, off, L):
            c = h * 256 + k2 * 128
            bgl = nb()
            mm(ps[bgl][:, 0:L], WGUB[0:16, c:c + 128], GLB[0:16, 0:L], True, True, r=['wgub', 'glb'],
               w=['ps%d' % bgl], inc=True)
            act(T[0][:, 0:L], ps[bgl][:, 0:L], AF.Sigmoid, r=['ps%d' % bgl, 'vecs'], w=['T0'],
                bias=vec(V_GB)[:, h * 2 + k2:h * 2 + k2 + 1])
            act(T[2][:, 0:L], T[0][:, 0:L], AF.Ln, r=['T0'], w=['T2'])
            wk_, wkk = next_w(win, 1024 + c)
            bkk = nb()
            proj(bkk, wk_, wkk, off, L)
            cp('dve', T[0][:, 0:L], ps[bkk][:, 0:L], r=['ps%d' % bkk], w=['T0'])

        def head_p1(h):
            bS = [nb()]
            bS.append(nb(excl=bS))
            for x_ in bS:
                held.add(x_)
            first = True
            for bi in (1, 0):
                off, L = BLKS[bi]
                glow_proj(off, L)
                for k2 in range(2):
                    gk_chain(h, k2, off, L)
                    pass1_decay(k2, bi, L, T[0][:, 0:L], ['T0'], 1.0 / 16.0)
                    k2_transposes(k2, L)
                vt_of = vtok_block([next_w(win, 2048 + h * 512 + vc * 128) for vc in range(4)], off, L, 4)
                for tl in range(4):
                    vt, vtk = vt_of(tl)
                    last = (bi == 0 and tl == 3)
                    for k2 in range(2):
                        mm(ps[bS[k2]][:, 0:512], k2t3_[:, tl, k2 * 128:(k2 + 1) * 128], vt[:, 0:512], first, last,
                           r=['k2t', vtk], w=['ps%d' % bS[k2]], inc=True)
                    first = False
            for k2 in range(2):
                cp('dve' if k2 else 'act', S0[:, k2, :], ps[bS[k2]][:, 0:512], r=['ps%d' % bS[k2]], w=['Sf0'])
                held.discard(bS[k2])
            b.dma('sp', cci[h * 256:(h + 1) * 256, :].rearrange("(kc p) v -> p kc v", p=128), S0[:, :, :],
                  r=['Sf0'], w=['cci'])

        def head_p2(h):
            b.dma('sp', S0[:, :, :], cco[h * 256:(h + 1) * 256, :].rearrange("(kc p) v -> p kc v", p=128),
                  w=['Sf0'], r=['cco'])
            ts(S0[:, :, :], S0[:, :, :], flag, None, ALU.mult, None, r=['Sf0', 'cst'], w=['Sf0'])
            sl0, sl0k = sbslot(0)
            cp('act', sl0, S0[:, :, :].rearrange("p a v -> p (a v)"), r=['Sf0'], w=[sl0k])
            g = 0
            for bi, (off, L) in enumerate(BLKS):
                samp = (bi == 2)
                slen = 8 if samp else 128
                glow_proj(off, L)
                for k2 in range(2):
                    c = h * 256 + k2 * 128
                    gk_chain(h, k2, off, L)
                    wq, wqk = next_w(win, c)
                    bq = nb()
                    proj(bq, wq, wqk, off, L)
                    held.add(bq)
                    decay_chain(k2, L, C_SEG8 if samp else C_SEG128, slen, T[0][:, 0:L], ['T0'], 1.0 / 16.0)
                    k2_transposes(k2, L)
                    held.discard(bq)
                    stt(qe3[:, k2, 0:L], ps[bq][:, 0:L], float(256 ** -0.5), T[2][:, 0:L], ALU.mult, ALU.mult,
                        r=['ps%d' % bq, 'T2'], w=['qe'])
                for vc in range(4):
                    wr_, wrk = next_w(win, 4096 + h * 512 + vc * 128)
                    br = nb()
                    proj(br, wr_, wrk, off, L)
                    act(ogh3[:, vc, off:off + L], ps[br][:, 0:L], AF.Silu, r=['ps%d' % br], w=['ogh'])
                vt_of = vtok_block([next_w(win, 2048 + h * 512 + vc * 128) for vc in range(4)], off, L, 4)
                if samp:
                    b.dma('sp', o_gp[0, h, :, :].rearrange("(kc p) v -> p kc v", p=128), S0[:, :, :], r=['Sf0'])
                for tl in range(L // 128):
                    vt, vtk = vt_of(tl)
                    osinks = []

                    def o_sink(vc, pso, pk, tl=tl, osinks=osinks):
                        osinks.append((vc, pso, pk))
                    if not samp:
                        bo_used = rec_tile_prompt(2, 4, tl, tl * 128, vt, vtk, 1, 128, cst[:, C_CM128:C_CM128 + 128],
                                                  C_SM128, tl, [(Sf[0], 'Sf0'), (Sf[1], 'Sf1')], sbslot, g, o_sink)
                        g += 1
                    else:
                        def get_state(jj):
                            q_ = 1 - jj % 2
                            S1 = Sf[q_]
                            src = st_g[0, jj, h, :, :].rearrange("(kc p) v -> p kc v", p=128)
                            b.dma('sp', S1[:, :, :], src, w=['Sf%d' % q_])
                            cp('act', SB_[q_][:, 0:1024], S1[:, :, :].rearrange("p a v -> p (a v)"), r=['Sf%d' % q_],
                               w=['sb%d' % q_])
                            return S1, 'Sf%d' % q_, SB_[q_], 'sb%d' % q_

                        def put_state(jj, done=False):
                            q_ = 1 - jj % 2
                            S1 = Sf[q_]
                            if done:
                                dst = o_gs[0, jj, h, :, :].rearrange("(kc p) v -> p kc v", p=128)
                                b.dma('sp', dst, S1[:, :, :], r=['Sf%d' % q_])
                                return None
                            return S1, 'Sf%d' % q_
                        bo_used = rec_tile(2, 4, tl, tl * 128, vt, vtk, (16, 8), cst[:, C_CM8:C_CM8 + 128], C_SM8, 0,
                                           False, get_state, put_state, o_sink)
                    c_ = off + tl * 128
                    bss = nb(excl=bo_used)
                    for (vc, pso, pk) in osinks:
                        act(SQB[:, vc * 128:(vc + 1) * 128], pso, AF.Square, r=[pk], w=['sqb'])
                    for vc in range(4):
                        mm(ps[bss][:, 0:128], ONB, SQB[:, vc * 128:(vc + 1) * 128], vc == 0, vc == 3,
                           r=['onb', 'sqb'], w=['ps%d' % bss], inc=(vc == 3))
                    act(T[0][:, 0:128], ps[bss][:, 0:128], AF.Ln, r=['ps%d' % bss, 'cst'], w=['T0'],
                        scale=1.0 / 512.0, bias=epsc)
                    act(T[0][:, 0:128], T[0][:, 0:128], AF.Exp, r=['T0'], w=['T0'], scale=-0.5)
                    for (vc, pso, pk) in osinks:
                        stt(T[1][:, 0:128], pso, vec(V_GON)[:, h * 4 + vc:h * 4 + vc + 1], T[0][:, 0:128],
                            ALU.mult, ALU.mult, r=[pk, 'vecs', 'T0'], w=['T1'])
                        tt(ogh3[:, vc, c_:c_ + 128], ogh3[:, vc, c_:c_ + 128], T[1][:, 0:128], ALU.mult,
                           r=['ogh', 'T1'], w=['ogh'])
            for oc in range(KC):
                s_ = wri[0] % 6
                wri[0] += 1
                src = wout[h * 512:(h + 1) * 512, oc * 128:(oc + 1) * 128].rearrange("(r p) n -> p r n", p=128)
                dst = WR[s_][:, 0:512].rearrange("p (r n) -> p r n", n=128)
                b.dma('pool', dst, src, w=['wr%d' % s_])
                for (off, L) in BLKS:
                    bk = nb()
                    for rt in range(4):
                        mm(ps[bk][:, 0:L], dst[:, rt, :], ogh3[:, rt, off:off + L], rt == 0, rt == 3,
                           r=['wr%d' % s_, 'ogh'], w=['ps%d' % bk], inc=(rt == 3))
                    tt(xs[:, oc, off:off + L], xs[:, oc, off:off + L], ps[bk][:, 0:L], ALU.add,
                       r=['xs', 'ps%d' % bk], w=['xs'])

        for h in range(4):
            head_p1(h)
        b.op('pool', lambda e: e.collective_compute("AllGather", ALU.bypass,
                                                    replica_groups=RG,
                                                    ins=[cci[:, :]], outs=[cco[:, :]]), r=['cci'], w=['cco'])
        for h in range(4):
            head_p2(h)

    def pool_mixer():
        wgp = W["pool_w_group"][0]
        cci = ccp_in.ap()
        cco = ccp_out.ap()
        for blk in BLKS:
            off, L = blk
            rms_rstd(blk, rsa[:, off:off + L], 'rsa', lambda kc: xs[:, kc, off:off + L], ['xs'], KC, float(D))
        gm = vec(V_NMIX + 2)
        XE = fa[:, 0:1039]
        XS_ = fa[:, 1040:1040 + 16 * 23]
        xs3 = XS_.rearrange("p (s l) -> p s l", l=23)
        WA = fa[:, 1408:1408 + 320]
        yp3 = YP.rearrange("p (a t) -> p a t", t=NT)
        pb3 = PB.rearrange("p (a t) -> p a t", t=240)

        def tok_major(c0):
            for g4 in range(4):
                bk = nb()
                for i in range(4):
                    kc = g4 * 4 + i
                    stt(osq[:, i * 128:(i + 1) * 128], xs[:, kc, c0:c0 + 128], gm[:, kc:kc + 1], rsa[:, c0:c0 + 128],
                        ALU.mult, ALU.mult, r=['xs', 'vecs', 'rsa'], w=['osq'])
                    tr(ps[bk][:, i * 128:(i + 1) * 128], osq[:, i * 128:(i + 1) * 128], r=['osq'], w=['ps%d' % bk])
                cp('act' if g4 % 2 else 'dve', fa[:, g4 * 512:(g4 + 1) * 512], ps[bk][:, :], r=['ps%d' % bk], w=TK)
        tok_major(1024)
        for s_ in range(16):
            b.dma('sp', o_ps[0, s_, 7:15, :], fa[s_ * 8:(s_ + 1) * 8, :], r=TK)
        b.dma('sp', o_ps[0, :, 0:7, :], st_p[0, :, 8:15, :])
        tok_major(896)
        b.dma('sp', o_pp[0, :, :], fa[113:128, :], r=TK)
        for kc in range(KC):
            stt(WA[:, 0:15], xs[:, kc, 1009:1024], gm[:, kc:kc + 1], rsa[:, 1009:1024], ALU.mult, ALU.mult,
                r=['xs', 'vecs', 'rsa'], w=TK)
            b.dma('sp', cci[:, kc * 15:(kc + 1) * 15], WA[:, 0:15], r=TK, w=['cci'])
        b.op('pool', lambda e: e.collective_compute("AllGather", ALU.bypass,
                                                    replica_groups=RG,
                                                    ins=[cci[:, :]], outs=[cco[:, :]]), r=['cci'], w=['cco'])
        for hh in range(2):
            b.dma('sp', fa[0:120, :], st_p[0, hh * 8:(hh + 1) * 8, :, :].rearrange("s r d -> (s r) d"), w=TK)
            for g4 in range(4):
                bk = nb()
                for i in range(4):
                    kc = g4 * 4 + i
                    b.op('pe', lambda e, o=ps[bk][:, i * 120:(i + 1) * 120], a=fa[0:120, kc * 128:(kc + 1) * 128]:
                         e.transpose(o, a, cst[0:120, C_ID:C_ID + 120]), r=TK + ['cst'], w=['ps%d' % bk])
                cp('act', pb3[:, g4 * 4:(g4 + 1) * 4, hh * 120:(hh + 1) * 120],
                   ps[bk][:, 0:480].rearrange("p (a n) -> p a n", n=120), r=['ps%d' % bk], w=['pb'])
        for g in range(4):
            w_ = (2, 4, 8, 16)[g]
            for ci in range(4):
                kc = g * 4 + ci
                b.dma('sp', XE[:, 0:15], cco[0:128, kc * 15:(kc + 1) * 15], r=['cco'], w=TK)
                ts(XE[:, 0:15], XE[:, 0:15], flag, None, ALU.mult, None, r=TK + ['cst'], w=TK)
                stt(XE[:, 15:1039], xs[:, kc, 0:1024], gm[:, kc:kc + 1], rsa[:, 0:1024], ALU.mult, ALU.mult,
                    r=['xs', 'vecs', 'rsa'], w=TK)
                cp('dve', xs3[:, :, 0:15], pb3[:, kc, :].rearrange("p (s r) -> p s r", r=15), r=['pb'], w=TK)
                stt(xs3[:, :, 15:23], xs[:, kc, 1024:1152].rearrange("p (s l) -> p s l", l=8), gm[:, kc:kc + 1],
                    rsa[:, 1024:1152].rearrange("p (s l) -> p s l", l=8), ALU.mult, ALU.mult,
                    r=['xs', 'vecs', 'rsa'], w=TK)
                acc = osq[:, 0:1024]
                tt(acc, XE[:, 15:1039], XE[:, 14:1038], ALU.add, r=TK, w=['osq'])
                for i in range(2, w_):
                    tt(acc, acc, XE[:, 15 - i:1039 - i], ALU.add, r=TK + ['osq'], w=['osq'])
                stt(yp3[:, ci, 0:1024], acc, 1.0 / w_, XE[:, 15:1039], ALU.mult, ALU.subtract, r=['osq'] + TK,
                    w=['yp'])
                tt(WA[:, 16:32], acc[:, 0:16], cst[:, C_INVC + g * 16:C_INVC + (g + 1) * 16], ALU.mult,
                   r=['osq', 'cst'], w=TK)
                tt(yp3[:, ci, 0:16], WA[:, 16:32], XE[:, 15:31], ALU.subtract, r=TK, w=['yp'])
                accs = osq[:, 1024:1152].rearrange("p (s l) -> p s l", l=8)
                tt(accs, xs3[:, :, 15:23], xs3[:, :, 14:22], ALU.add, r=TK, w=['osq'])
                for i in range(2, w_):
                    tt(accs, accs, xs3[:, :, 15 - i:23 - i], ALU.add, r=TK + ['osq'], w=['osq'])
                stt(yp3[:, ci, 1024:1152].rearrange("p (s l) -> p s l", l=8), accs, 1.0 / w_, xs3[:, :, 15:23],
                    ALU.mult, ALU.subtract, r=['osq'] + TK, w=['yp'])
            for oc4 in range(4):
                oc = g * 4 + oc4
                s = wri[0] % 6
                wri[0] += 1
                src = wgp[g, :, oc4 * 128:(oc4 + 1) * 128].rearrange("(r p) n -> p r n", p=128)
                dst = WR[s][:, 0:512].rearrange("p (r n) -> p r n", n=128)
                b.dma('pool', dst, src, w=['wr%d' % s])
                for (off, L) in BLKS:
                    bk = nb()
                    for rt in range(4):
                        mm(ps[bk][:, 0:L], dst[:, rt, :], yp3[:, rt, off:off + L], rt == 0, rt == 3,
                           r=['wr%d' % s, 'yp'], w=['ps%d' % bk], inc=(rt == 3))
                    stt(xs[:, oc, off:off + L], ps[bk][:, 0:L], vec(V_PSC)[:, oc:oc + 1], xs[:, oc, off:off + L],
                        ALU.mult, ALU.add, r=['ps%d' % bk, 'vecs', 'xs'], w=['xs'])

    def final_store():
        for blk in BLKS:
            off, L = blk
            rms_rstd(blk, rsa[:, off:off + L], 'rsa', lambda kc: xs[:, kc, off:off + L], ['xs'], KC, float(D))
        gf = vec(V_FIN)
        for tt_ in range(9):
            c0 = tt_ * 128
            for g4 in range(4):
                bk = nb()
                for i in range(4):
                    kc = g4 * 4 + i
                    stt(osq[:, i * 128:(i + 1) * 128], xs[:, kc, c0:c0 + 128], gf[:, kc:kc + 1], rsa[:, c0:c0 + 128],
                        ALU.mult, ALU.mult, r=['xs', 'vecs', 'rsa'], w=['osq'])
                    tr(ps[bk][:, i * 128:(i + 1) * 128], osq[:, i * 128:(i + 1) * 128], r=['osq'], w=['ps%d' % bk])
                cp('act' if g4 % 2 else 'dve', fa[:, g4 * 512:(g4 + 1) * 512], ps[bk][:, :], r=['ps%d' % bk],
                   w=[TK[g4]])
            b.dma('sp', y_out[c0:c0 + 128, :], fa[:, :], r=TK)

    cp('dve', ONB, ones_f, r=['cst'], w=['onb'])
    lower_bounds()
    load_x()
    if stages is None:
        stages = []
        for li in range(4):
            stages += ["ffn1:%d" % li, ("hgrn:%d" % li) if li % 3 == 0 else ("gla:%d" % li if li % 3 == 1 else "pool:%d" % li),
                       "ffn2:%d" % li]
    for st in stages:
        kind, li = st.split(":")
        li = int(li)
        if kind == "ffn1":
            ffn(li, 1)
        elif kind == "ffn2":
            ffn(li, 2)
        elif kind == "hgrn":
            hgrn(li, li // 3)
        elif kind == "gla":
            gla()
        elif kind == "pool":
            pool_mixer()
    final_store()
    b.finish()

    with nc.Block() as block:
        @block.tensor
        def _(e):
            b.replay('pe', e)

        @block.scalar
        def _(e):
            b.replay('act', e)

        @block.vector
        def _(e):
            b.replay('dve', e)

        @block.gpsimd
        def _(e):
            b.replay('pool', e)

        @block.sync
        def _(e):
            b.replay('sp', e)
    es.close()
    nc._used_w = list(W.keys())
    return nc


def _consts(core):
    c = np.zeros((128, C_END), np.float32)
    s = np.arange(128)[:, None]
    t = np.arange(128)[None, :]
    c[:, C_ID:C_ID + 128] = np.eye(128)
    c[:, C_CM32:C_CM32 + 128] = ((s // 32 == t // 32) & (s <= t))
    c[:, C_CM8:C_CM8 + 128] = ((s // 8 == t // 8) & (s <= t))
    c[:, C_CM128:C_CM128 + 128] = (s <= t)
    for j in range(4):
        c[:, C_SM32 + j] = (np.arange(128) // 32 == j)
    for j in range(16):
        c[:, C_SM8 + j] = (np.arange(128) // 8 == j)
    c[:, C_SM128] = 1.0
    c[:, C_SEG32:C_SEG32 + 512] = (np.arange(512) % 32 != 0)[None, :]
    c[:, C_SEG8:C_SEG8 + 128] = (np.arange(128) % 8 != 0)[None, :]
    c[:, C_SEG128:C_SEG128 + 512] = (np.arange(512) % 128 != 0)[None, :]
    c[:, C_ONES:C_ONES + 128] = 1.0
    half = core % 2
    c[:, C_FLAG] = float(half)
    c[:, C_EPS] = EPS
    for g, w in enumerate((2, 4, 8, 16)):
        tpos = np.arange(16) + half * 1024
        c[:, C_INVC + g * 16:C_INVC + (g + 1) * 16] = (1.0 / np.minimum(w, tpos + 1))[None, :]
        c[:, C_INVW + g] = 1.0 / w
    return c


def _fm(v):
    return np.ascontiguousarray(np.asarray(v, np.float32).reshape(16, 128).T)


_NC_CACHE = {}


def kernel(**inp):
    f32 = lambda a: np.ascontiguousarray(np.asarray(a, dtype=np.float32))
    x_prompt = f32(inp["x_prompt"]); x_sample = f32(inp["x_sample"])
    vl = []
    for i in range(4):
        vl.append(_fm(inp["norm_ffn1"][i]))
    for i in range(4):
        vl.append(_fm(inp["norm_mix"][i]))
    for i in range(4):
        vl.append(_fm(inp["norm_ffn2"][i]))
    vl.append(_fm(inp["final_norm"]))
    vl.append(_fm(inp["hgrn_o_norm"][0])); vl.append(_fm(inp["hgrn_o_norm"][1]))
    vl.append(_fm(inp["gla_o_norm"][0]))
    gb = np.zeros((128, 16), np.float32)
    gb[:, 0:8] = np.asarray(inp["gla_b_gate"][0], np.float32).reshape(8, 128).T
    vl.append(gb)
    vl.append(_fm(inp["pool_scale"][0]))
    for i in range(4):
        vl.append(_fm(inp["hgrn_lb_logits"][i]))
    vecs = np.ascontiguousarray(np.stack(vl, axis=1).reshape(128, NV * 16))
    wnames = ["ffn1_w_gate", "ffn1_w_up", "ffn1_w_down", "ffn2_w_gate", "ffn2_w_up", "ffn2_w_down",
              "hgrn_w_in", "hgrn_w_out", "gla_w_in", "gla_w_gate_up", "gla_w_out", "pool_w_group"]
    wts = {n: f32(inp[n]) for n in wnames}
    st_h = f32(inp["state_hgrn"]); st_g = f32(inp["state_gla"]); st_p = f32(inp["state_pool"])
    in_maps = []
    for c in range(NCORES):
        bq, half = c // 2, c % 2
        xin = np.concatenate([x_prompt[bq, half * 1024:(half + 1) * 1024], x_sample[c * 16:(c + 1) * 16].reshape(128, D)], 0)
        m = {"x_in": np.ascontiguousarray(xin), "st_h": np.ascontiguousarray(st_h[:, c * 16:(c + 1) * 16]),
             "st_g": np.ascontiguousarray(st_g[:, c * 16:(c + 1) * 16]),
             "st_p": np.ascontiguousarray(st_p[:, c * 16:(c + 1) * 16]),
             "cst": _consts(c), "vecs": vecs}
        m.update(wts)
        in_maps.append(m)
    if "nc" not in _NC_CACHE:
        _NC_CACHE["nc"] = build_program()
    res = run_bass_kernel_spmd(_NC_CACHE["nc"], in_maps, core_ids=list(range(NCORES)))
    R = res.results
    y_prompt = np.zeros((4, 2048, D), np.float32)
    y_sample = np.zeros((128, 8, D), np.float32)
    hp = np.zeros((2, 4, 16, 128, 128), np.float32)
    gp = np.zeros((1, 4, 4, 256, 512), np.float32)
    pp = np.zeros((1, 4, 15, D), np.float32)
    hs = np.zeros((2, 128, 16, 128, 128), np.float32)
    gs = np.zeros((1, 128, 4, 256, 512), np.float32)
    pss = np.zeros((1, 128, 15, D), np.float32)
    for c in range(NCORES):
        bq, half = c // 2, c % 2
        r = R[c]
        y_prompt[bq, half * 1024:(half + 1) * 1024] = r["y"][0:1024]
        y_sample[c * 16:(c + 1) * 16] = r["y"][1024:1152].reshape(16, 8, D)
        if half == 1:
            hp[:, bq] = r["o_hp"]
            gp[:, bq] = r["o_gp"]
            pp[:, bq] = r["o_pp"]
        hs[:, c * 16:(c + 1) * 16] = r["o_hs"]
        gs[:, c * 16:(c + 1) * 16] = r["o_gs"]
        pss[:, c * 16:(c + 1) * 16] = r["o_ps"]
    return (y_prompt, y_sample, hp, gp, pp, hs, gs, pss)
```
